# Optimizing a Trainium2 kernel written in Bass

```python
import math
import jax
import jax.numpy as jnp
from jax import lax
import numpy as np

D_MODEL = 2048
BATCH = 4
SEQ = 4096
DEPTH = 1

GRID_W = 64
CTX_LEN = 256
D_MIX = D_MODEL
D_RWKV = D_MIX // 2
RWKV_HEAD = 64
N_RWKV_HEADS = D_RWKV // RWKV_HEAD
DECAY_LORA = 64
AAA_LORA = 64
GATE_LORA = 160
D_S5 = D_MIX - D_RWKV
S5_GROUP = 16
N_S5_GROUPS = D_S5 // S5_GROUP
S5_STATE = 64
D_FF = 5632
N_MOD = 6
RMS_EPS = 1e-6
LNX_EPS = 64e-5
RWKV_SIZES = (D_RWKV, D_RWKV, D_RWKV, GATE_LORA, DECAY_LORA, DECAY_LORA, AAA_LORA, AAA_LORA)
RWKV_IN = 3 * D_RWKV + GATE_LORA + 2 * DECAY_LORA + 2 * AAA_LORA
N_IN = RWKV_IN + D_S5

kernel_name = "hybrid_rwkv7_s5_prefix_dit_block"


def _rms_norm(x, g):
    xf = x.astype(jnp.float32)
    y = xf * lax.rsqrt(jnp.mean(xf * xf, axis=-1, keepdims=True) + RMS_EPS)
    return (y * g.astype(jnp.float32)).astype(x.dtype)


def _modulate(h, shift, scale):
    return h * (1.0 + scale) + shift


def _neighbours(p, rows):
    b, t, ch = p.shape
    q = p.reshape(b, rows, t // rows, ch)
    q = jnp.pad(q, ((0, 0), (0, 0), (1, 1), (0, 0)))
    return q[:, :, :-2].reshape(b, t, ch), q[:, :, 2:].reshape(b, t, ch)


def _split_cols(z, sizes):
    out, start = [], 0
    for s in sizes:
        out.append(z[..., start:start + s])
        start += s
    return out


def _wkv7(r, decay, k, v, kk, a, s0, reverse, want_output):
    tm = lambda z: jnp.moveaxis(z, 1, 0)
    xs = (tm(decay), tm(k), tm(v), tm(kk), tm(kk * a))
    if want_output:
        xs = xs + (tm(r),)

    def step(s, inp):
        w_t, k_t, v_t, kk_t, b_t = inp[:5]
        sa = jnp.einsum('bhvk,bhk->bhv', s, kk_t)
        s = s * w_t[:, :, None, :] - sa[..., None] * b_t[:, :, None, :] + v_t[..., None] * k_t[:, :, None, :]
        y = jnp.einsum('bhvk,bhk->bhv', s, inp[5]) if want_output else None
        return s, y

    s_fin, ys = lax.scan(step, s0, xs, reverse=reverse)
    return (jnp.moveaxis(ys, 0, 1) if want_output else None), s_fin


def _zoh(a_re, a_im, log_step, b_re, b_im):
    f32 = jnp.float32
    a_re, a_im = a_re.astype(f32), a_im.astype(f32)
    b_re, b_im = b_re.astype(f32), b_im.astype(f32)
    dt = jnp.exp(log_step.astype(f32))[:, None]
    mag = jnp.exp(a_re * dt)
    ang = a_im * dt
    lam_re, lam_im = mag * jnp.cos(ang), mag * jnp.sin(ang)
    den = a_re * a_re + a_im * a_im
    nr = lam_re - 1.0
    f_re = (nr * a_re + lam_im * a_im) / den
    f_im = (lam_im * a_re - nr * a_im) / den
    bb_re = f_re[..., None] * b_re - f_im[..., None] * b_im
    bb_im = f_re[..., None] * b_im + f_im[..., None] * b_re
    return lam_re, lam_im, bb_re, bb_im


def _s5_scan(bu_re, bu_im, lam_re, lam_im, s0, reverse):
    a_re = jnp.broadcast_to(lam_re, bu_re.shape)
    a_im = jnp.broadcast_to(lam_im, bu_im.shape)

    def combine(e1, e2):
        a1r, a1i, b1r, b1i = e1
        a2r, a2i, b2r, b2i = e2
        return (a2r * a1r - a2i * a1i, a2r * a1i + a2i * a1r,
                a2r * b1r - a2i * b1i + b2r, a2r * b1i + a2i * b1r + b2i)

    ar, ai, sr, si = lax.associative_scan(combine, (a_re, a_im, bu_re, bu_im), reverse=reverse, axis=1)
    if s0 is not None:
        s0r, s0i = s0[0][:, None], s0[1][:, None]
        sr, si = sr + ar * s0r - ai * s0i, si + ar * s0i + ai * s0r
    return sr, si


def _token_mixer(h, rows, lp, init, want_output, want_states):
    f32 = jnp.float32
    bsz, t, _ = h.shape
    p = jnp.einsum('btd,dn->btn', h, lp['w_in'])
    q = p[..., :RWKV_IN]
    prev, nxt = _neighbours(q, rows)
    mu = lp['shift_mu']
    q = (q + mu[0] * (prev - q) + mu[1] * (nxt - q)).astype(f32)
    u = p[..., RWKV_IN:].astype(f32).reshape(bsz, t, N_S5_GROUPS, S5_GROUP)

    r, k, v, gd, wd_f, wd_b, ad_f, ad_b = _split_cols(q, RWKV_SIZES)
    heads = lambda z: z.reshape(bsz, t, N_RWKV_HEADS, RWKV_HEAD)
    kk = heads(k * lp['rwkv_k_k'])
    kk = kk / jnp.maximum(jnp.sqrt(jnp.sum(kk * kk, axis=-1, keepdims=True)), 1e-12)
    rh, vh = heads(r), heads(v)
    rwkv_y, rwkv_fin = [], []
    for d, (wd, ad) in enumerate(((wd_f, ad_f), (wd_b, ad_b))):
        w_log = -jax.nn.softplus(-(lp['rwkv_w0'][d] + jnp.tanh(wd) @ lp['rwkv_w_up'][d])) - 0.5
        decay = jnp.exp(-jnp.exp(w_log))
        a = jax.nn.sigmoid(lp['rwkv_a0'][d] + ad @ lp['rwkv_a_up'][d])
        kd = k * (1.0 + (a - 1.0) * lp['rwkv_k_a'])
        if init is None:
            s0 = jnp.zeros((bsz, N_RWKV_HEADS, RWKV_HEAD, RWKV_HEAD), f32)
        else:
            s0 = init['rwkv'][d]
        y, s_fin = _wkv7(rh, heads(decay), heads(kd), vh, kk, heads(a), s0, d == 1, want_output)
        rwkv_y.append(y)
        rwkv_fin.append(s_fin)

    s5_y, s5_fin = [], []
    for d in range(2):
        lam_re, lam_im, bb_re, bb_im = _zoh(lp['s5_a_re'][d], lp['s5_a_im'][d], lp['s5_log_step'][d],
                                            lp['s5_b_re'][d], lp['s5_b_im'][d])
        bu_re = jnp.einsum('gph,btgh->btgp', bb_re, u)
        bu_im = jnp.einsum('gph,btgh->btgp', bb_im, u)
        s0 = None if init is None else init['s5'][d]
        sr, si = _s5_scan(bu_re, bu_im, lam_re, lam_im, s0, d == 1)
        if want_output:
            s5_y.append(jnp.einsum('ghp,btgp->btgh', lp['s5_c_re'][d], sr)
                        - jnp.einsum('ghp,btgp->btgh', lp['s5_c_im'][d], si))
        if want_states:
            idx = -1 if d == 0 else 0
            s5_fin.append((sr[:, idx], si[:, idx]))

    states = {'rwkv': rwkv_fin, 's5': s5_fin} if want_states else None
    if not want_output:
        return None, states

    yh = rwkv_y[0] + rwkv_y[1]
    mean = jnp.mean(yh, axis=-1, keepdims=True)
    var = jnp.mean(jnp.square(yh - mean), axis=-1, keepdims=True)
    yn = ((yh - mean) * lax.rsqrt(var + LNX_EPS)).reshape(bsz, t, D_RWKV) * lp['lnx_w'] + lp['lnx_b']
    bonus = (jnp.sum(rh * heads(k) * lp['rwkv_r_k'], axis=-1, keepdims=True) * vh).reshape(bsz, t, D_RWKV)
    g = jax.nn.sigmoid(gd) @ lp['rwkv_g_up']
    o_rwkv = (yn + bonus) * g

    y5 = (s5_y[0] + s5_y[1] + lp['s5_d'].reshape(N_S5_GROUPS, S5_GROUP) * u).reshape(bsz, t, D_S5)
    z = jax.nn.gelu(y5)
    o_s5 = z * jax.nn.sigmoid(z @ lp['s5_glu_w'] + lp['s5_glu_b'])

    o = jnp.concatenate([o_rwkv, o_s5], axis=-1).astype(h.dtype)
    return jnp.einsum('btm,md->btd', o, lp['w_out']), states


def _conv_ffn(h, rows, w_up, conv_w, conv_b, w_down):
    up = jnp.einsum('btd,df->btf', h, w_up)
    gate, val = up[..., :D_FF], up[..., D_FF:]
    prev, nxt = _neighbours(gate, rows)
    gate = conv_w[0] * prev + conv_w[1] * gate + conv_w[2] * nxt + conv_b
    return jnp.einsum('btf,fd->btd', jax.nn.gelu(gate) * val, w_down)


def setup_inputs(seed: int = 0) -> dict:
    key = jax.random.key(seed)
    ks = jax.random.split(key, 40)
    f32 = jnp.float32
    L, G, P, H, N = DEPTH, N_S5_GROUPS, S5_STATE, N_RWKV_HEADS, RWKV_HEAD

    def nrm(i, shape, scale):
        return jax.random.normal(ks[i], shape, f32) * scale

    return {
        'x': nrm(0, (BATCH, SEQ, D_MODEL), 1.0),
        'c': nrm(1, (BATCH, D_MODEL), 1.0),
        'ctx': nrm(2, (BATCH, CTX_LEN, D_MODEL), 1.0),
        'c_ctx': nrm(3, (D_MODEL,), 1.0),
        'mod_w': nrm(4, (L, D_MODEL, N_MOD * D_MODEL), 0.5 * D_MODEL ** -0.5),
        'mod_b': nrm(5, (L, N_MOD * D_MODEL), 0.02),
        'norm_mix_g': 1.0 + nrm(6, (L, D_MODEL), 0.02),
        'w_in': nrm(7, (L, D_MODEL, N_IN), D_MODEL ** -0.5),
        'shift_mu': jax.random.uniform(ks[8], (L, 2, RWKV_IN), f32, 0.0, 0.5),
        'rwkv_w0': jax.random.uniform(ks[9], (L, 2, D_RWKV), f32, -6.5, -1.5),
        'rwkv_w_up': nrm(10, (L, 2, DECAY_LORA, D_RWKV), 0.5 * DECAY_LORA ** -0.5),
        'rwkv_a0': nrm(11, (L, 2, D_RWKV), 0.1),
        'rwkv_a_up': nrm(12, (L, 2, AAA_LORA, D_RWKV), 0.5 * AAA_LORA ** -0.5),
        'rwkv_g_up': nrm(13, (L, GATE_LORA, D_RWKV), GATE_LORA ** -0.5),
        'rwkv_k_k': 0.85 + nrm(14, (L, D_RWKV), 0.02),
        'rwkv_k_a': 1.0 + nrm(15, (L, D_RWKV), 0.02),
        'rwkv_r_k': nrm(16, (L, H, N), 0.1),
        'lnx_w': 1.0 + nrm(17, (L, D_RWKV), 0.02),
        'lnx_b': nrm(18, (L, D_RWKV), 0.02),
        's5_a_re': -0.5 + nrm(19, (L, 2, G, P), 0.01),
        's5_a_im': jnp.pi * jnp.arange(P, dtype=f32) + nrm(20, (L, 2, G, P), 0.01),
        's5_log_step': jax.random.uniform(ks[21], (L, 2, G), f32, math.log(1e-3), math.log(1e-1)),
        's5_b_re': nrm(22, (L, 2, G, P, S5_GROUP), (2 * S5_GROUP) ** -0.5),
        's5_b_im': nrm(23, (L, 2, G, P, S5_GROUP), (2 * S5_GROUP) ** -0.5),
        's5_c_re': nrm(24, (L, 2, G, S5_GROUP, P), P ** -0.5),
        's5_c_im': nrm(25, (L, 2, G, S5_GROUP, P), P ** -0.5),
        's5_d': nrm(26, (L, D_S5), 1.0),
        's5_glu_w': nrm(27, (L, D_S5, D_S5), D_S5 ** -0.5),
        's5_glu_b': nrm(28, (L, D_S5), 0.02),
        'w_out': nrm(29, (L, D_MIX, D_MODEL), D_MIX ** -0.5),
        'norm_ffn_g': 1.0 + nrm(30, (L, D_MODEL), 0.02),
        'ffn_w_up': nrm(31, (L, D_MODEL, 2 * D_FF), D_MODEL ** -0.5),
        'ffn_conv_w': nrm(32, (L, 3, D_FF), 3.0 ** -0.5),
        'ffn_conv_b': nrm(33, (L, D_FF), 0.02),
        'ffn_w_down': nrm(34, (L, D_FF, D_MODEL), D_FF ** -0.5),
        'final_norm_g': 1.0 + nrm(35, (D_MODEL,), 0.02),
    }


def reference(x, c, ctx, c_ctx, mod_w, mod_b, norm_mix_g, w_in, shift_mu, rwkv_w0, rwkv_w_up, rwkv_a0,
              rwkv_a_up, rwkv_g_up, rwkv_k_k, rwkv_k_a, rwkv_r_k, lnx_w, lnx_b, s5_a_re, s5_a_im,
              s5_log_step, s5_b_re, s5_b_im, s5_c_re, s5_c_im, s5_d, s5_glu_w, s5_glu_b, w_out,
              norm_ffn_g, ffn_w_up, ffn_conv_w, ffn_conv_b, ffn_w_down, final_norm_g):
    rows = x.shape[1] // GRID_W
    for i in range(DEPTH):
        lp = {
            'w_in': w_in[i], 'shift_mu': shift_mu[i],
            'rwkv_w0': rwkv_w0[i], 'rwkv_w_up': rwkv_w_up[i], 'rwkv_a0': rwkv_a0[i], 'rwkv_a_up': rwkv_a_up[i],
            'rwkv_g_up': rwkv_g_up[i], 'rwkv_k_k': rwkv_k_k[i], 'rwkv_k_a': rwkv_k_a[i], 'rwkv_r_k': rwkv_r_k[i],
            'lnx_w': lnx_w[i], 'lnx_b': lnx_b[i],
            's5_a_re': s5_a_re[i], 's5_a_im': s5_a_im[i], 's5_log_step': s5_log_step[i],
            's5_b_re': s5_b_re[i], 's5_b_im': s5_b_im[i], 's5_c_re': s5_c_re[i], 's5_c_im': s5_c_im[i],
            's5_d': s5_d[i], 's5_glu_w': s5_glu_w[i], 's5_glu_b': s5_glu_b[i], 'w_out': w_out[i],
        }
        last = i == DEPTH - 1
        mod_x = (jax.nn.silu(c) @ mod_w[i] + mod_b[i])[:, None, :]
        mod_c = (jax.nn.silu(c_ctx)[None] @ mod_w[i] + mod_b[i])[:, None, :]
        sh_x, sc_x, gt_x, shf_x, scf_x, gtf_x = jnp.split(mod_x, N_MOD, axis=-1)
        sh_c, sc_c, gt_c, shf_c, scf_c, gtf_c = jnp.split(mod_c, N_MOD, axis=-1)

        hc = _modulate(_rms_norm(ctx, norm_mix_g[i]), sh_c, sc_c)
        ctx_mix, ctx_states = _token_mixer(hc, 1, lp, None, not last, True)

        hx = _modulate(_rms_norm(x, norm_mix_g[i]), sh_x, sc_x)
        x_mix, _ = _token_mixer(hx, rows, lp, ctx_states, True, False)
        x = x + gt_x * x_mix
        hx = _modulate(_rms_norm(x, norm_ffn_g[i]), shf_x, scf_x)
        x = x + gtf_x * _conv_ffn(hx, rows, ffn_w_up[i], ffn_conv_w[i], ffn_conv_b[i], ffn_w_down[i])

        if not last:
            ctx = ctx + gt_c * ctx_mix
            hc = _modulate(_rms_norm(ctx, norm_ffn_g[i]), shf_c, scf_c)
            ctx = ctx + gtf_c * _conv_ffn(hc, 1, ffn_w_up[i], ffn_conv_w[i], ffn_conv_b[i], ffn_w_down[i])
    return _rms_norm(x, final_norm_g)
```

```python
import numpy as np
import concourse.bass as bass
import concourse.mybir as mybir
from concourse.bass_utils import run_bass_kernel_spmd

F32 = mybir.dt.float32
BF16 = mybir.dt.bfloat16
ALU = mybir.AluOpType
AF = mybir.ActivationFunctionType

DEBUG = False
STOP_AFTER = 99
SKIP_RWKV = False
S5_MB = 8
RWKV_BF16 = False
SAME_ENGINE_INORDER = ("pe",)

T = 4352
D = 2048
NIN = 4512
TILES = [(0, 256)] + [(256 + 512 * i, 512) for i in range(8)]
OWN0, OWN1 = 256, 2304
NOWN = 2048
DFF = 5632
CHUNKS = [(128 * i, 128, "rkv") for i in range(24)]
CHUNKS += [(3072, 128, "gd0"), (3200, 32, "gd1"), (3232, 64, "wd0"), (3296, 64, "wd1"),
           (3360, 64, "ad0"), (3424, 64, "ad1")]
CHUNKS += [(3488 + 128 * i, 128, "u") for i in range(8)]


class Buf:
    __slots__ = ("name", "w", "r")

    def __init__(self, name=""):
        self.name = name
        self.w = None
        self.r = []


class Sched:
    EPOCH = 30000
    NDMA = 6

    def __init__(self, nc):
        self.nc = nc
        self.engines = ["pe", "dve", "act", "pool", "sp"]
        self.stream = {e: [] for e in self.engines}
        self.cnt = {e: 0 for e in self.engines}
        self.sems = {}
        self.seen = {e: {} for e in self.engines}
        self.dma_n = {e: 0 for e in self.engines}
        self.dma_tok = {e: [None] * self.NDMA for e in self.engines}
        self.last = {}
        self.pending = {e: [] for e in self.engines}

    def barrier(self):
        toks = list(self.last.values())
        for e in self.engines:
            self.pending[e] = list(toks)

    def _sem(self, key):
        if key not in self.sems:
            self.sems[key] = self.nc.alloc_semaphore(name="s_%s_%s" % key)
        return self.sems[key]

    def _waits(self, eng, deps):
        need = {}
        for t in deps:
            if t is None:
                continue
            key, val, teng, is_dma = t
            if (not is_dma) and teng == eng and eng in SAME_ENGINE_INORDER:
                continue
            if self.seen[eng].get(key, 0) >= val:
                continue
            if need.get(key, 0) < val:
                need[key] = val
        for key, val in need.items():
            self.seen[eng][key] = val
        return [(self._sem(key), val) for key, val in need.items()]

    @staticmethod
    def _deps(reads, writes):
        deps = []
        for b in reads:
            deps.append(b.w)
        for b in writes:
            deps.append(b.w)
            deps.extend(b.r)
        return deps

    def _commit(self, tok, reads, writes):
        for b in reads:
            b.r.append(tok)
            if len(b.r) > 48:
                b.r = b.r[-48:]
        for b in writes:
            b.w = tok
            b.r = []
        self.last[tok[0]] = tok

    def op(self, eng, fn, reads=(), writes=()):
        deps = self._deps(reads, writes)
        if self.pending[eng]:
            deps.extend(self.pending[eng])
            self.pending[eng] = []
        waits = self._waits(eng, deps)
        n = self.cnt[eng]
        key = (eng, n // self.EPOCH)
        val = n % self.EPOCH + 1
        self.cnt[eng] = n + 1
        self.stream[eng].append((waits, fn, self._sem(key), 1))
        tok = (key, val, eng, False)
        self._commit(tok, reads, writes)
        return tok

    def dma(self, eng, fn, reads=(), writes=()):
        i = self.dma_n[eng]
        self.dma_n[eng] = i + 1
        slot = i % self.NDMA
        deps = self._deps(reads, writes)
        if self.pending[eng]:
            deps.extend(self.pending[eng])
            self.pending[eng] = []
        deps.append(self.dma_tok[eng][slot])
        waits = self._waits(eng, deps)
        key = ("d" + eng, slot)
        val = 16 * (i // self.NDMA + 1)
        self.stream[eng].append((waits, fn, self._sem(key), 16))
        tok = (key, val, eng, True)
        self.dma_tok[eng][slot] = tok
        self._commit(tok, reads, writes)
        return tok

    def finish(self, final_eng="sp"):
        waits = self._waits(final_eng, list(self.last.values()))
        self.stream[final_eng].append((waits, None, None, 0))
        streams = self.stream

        def replay(e, name):
            for waits, fn, sem, inc in streams[name]:
                for s, v in waits:
                    e.wait_ge(s, v)
                if fn is not None:
                    fn(e).then_inc(sem, inc)

        with self.nc.Block() as block:
            @block.tensor
            def _(e):
                replay(e, "pe")

            @block.vector
            def _(e):
                replay(e, "dve")

            @block.scalar
            def _(e):
                replay(e, "act")

            @block.gpsimd
            def _(e):
                replay(e, "pool")

            @block.sync
            def _(e):
                replay(e, "sp")


class Arena:
    def __init__(self, ap, width, sched=None):
        self.ap = ap
        self.width = width
        self.off = 0
        self.sched = sched

    def reset(self):
        self.off = 0
        if self.sched is not None:
            self.sched.barrier()

    def f32(self, n):
        nr = (n + 7) // 8 * 8
        assert self.off + nr <= self.width, ("arena overflow", self.off, nr, self.width)
        a = self.ap[:, self.off:self.off + n]
        self.off += nr
        return a

    def bf16(self, n):
        return self.f32((n + 1) // 2).bitcast(BF16)[:, 0:n]


class G:
    pass


def MM(S, out, lhsT, rhs, start, stop, reads, writes):
    return S.op("pe", lambda e: e.matmul(out, lhsT=lhsT, rhs=rhs, start=start, stop=stop), reads, writes)


def TR(S, out, in_, ident, reads, writes):
    return S.op("pe", lambda e: e.transpose(out, in_, ident), reads, writes)


def ACT(S, out, in_, func, reads, writes, bias=0.0, scale=1.0, accum_out=None, eng="act"):
    if accum_out is None:
        return S.op(eng, lambda e: e.activation(out=out, in_=in_, func=func, bias=bias, scale=scale), reads, writes)
    return S.op(eng, lambda e: e.activation(out=out, in_=in_, func=func, bias=bias, scale=scale,
                                            accum_out=accum_out), reads, writes)


def TS(S, eng, out, in0, s1, s2, op0, op1, reads, writes):
    if s2 is None:
        return S.op(eng, lambda e: e.tensor_scalar(out, in0, s1, None, op0), reads, writes)
    return S.op(eng, lambda e: e.tensor_scalar(out, in0, s1, s2, op0, op1), reads, writes)


def TT(S, eng, out, in0, in1, op, reads, writes):
    return S.op(eng, lambda e: e.tensor_tensor(out, in0, in1, op), reads, writes)


def STT(S, eng, out, in0, scalar, in1, op0, op1, reads, writes):
    return S.op(eng, lambda e: e.scalar_tensor_tensor(out, in0, scalar, in1, op0, op1), reads, writes)


def CP(S, eng, out, in_, reads, writes):
    if eng == "act":
        return S.op(eng, lambda e: e.copy(out, in_), reads, writes)
    return S.op(eng, lambda e: e.tensor_copy(out, in_), reads, writes)


def DMA(S, q, out, in_, reads, writes, slow=False):
    q = "pool" if str(out.space).endswith("DRAM") else "sp"
    if slow:
        return S.dma(q, lambda e: e.dma_start(out=out, in_=in_, allow_slow_non_contiguous=True), reads, writes)
    return S.dma(q, lambda e: e.dma_start(out=out, in_=in_), reads, writes)


def scratch(g, name, shape, dt):
    kind = "ExternalOutput" if (DEBUG and name in g.dbg) else "Internal"
    t = g.nc.dram_tensor(name, list(shape), dt, kind=kind).ap()
    g.scr[name] = t
    g.sb[name] = Buf(name)
    return t


def colvec(g, dram2d, rows, dst, q="sp"):
    S = g.S
    st = g.cv_stage
    DMA(S, q, st[0:rows, :], dram2d, [], [g.b_cvs])
    TR(S, g.ps[:, 3584:3584 + rows], st[0:rows, :], g.ident[0:rows, 0:rows], [g.b_cvs, g.b_const], [g.b_cvp])
    CP(S, "dve", dst, g.ps[:, 3584:3584 + rows], [g.b_cvp], [g.b_pers])


def stage_mod(g):
    S, nc, A = g.S, g.nc, g.arena
    A.reset()
    I = g.inp
    sc = g.pers.f32(32)
    sc3 = sc.rearrange("p (k t) -> p k t", t=2)
    tmp = g.pers.f32(32)
    colvec(g, I["c"].rearrange("(k p) -> k p", p=128), 16, tmp[:, 0:16])
    colvec(g, I["c_ctx"].rearrange("(k p) -> k p", p=128), 16, tmp[:, 16:32])
    ACT(S, sc3[:, :, 0], tmp[:, 0:16], AF.Silu, [g.b_pers], [g.b_pers])
    ACT(S, sc3[:, :, 1], tmp[:, 16:32], AF.Silu, [g.b_pers], [g.b_pers])
    modb = g.pers.f32(96)
    colvec(g, I["mod_b"].rearrange("(k p) -> k p", p=128), 96, modb)
    wv = I["mod_w"].rearrange("(kc p) n -> p kc n", p=128)
    NB = 3
    wt = [A.f32(16 * 512) for _ in range(NB)]
    bw = [Buf() for _ in range(NB)]
    bps = Buf()
    pm = g.ps[:, 0:192]
    rowm = A.f32(6 * D)
    brow = Buf()
    bkm = [Buf() for _ in range(4)]
    for nb in range(24):
        w = wt[nb % NB]
        w3 = w.rearrange("p (k n) -> p k n", n=512)
        DMA(S, "sp", w3, wv[:, :, nb * 512:(nb + 1) * 512], [], [bw[nb % NB]])
        pbk = g.ps[0:2, 512 * (1 + nb % 4):512 * (2 + nb % 4)]
        for kc in range(16):
            MM(S, pbk, sc3[:, kc, :], w3[:, kc, :], kc == 0, kc == 15, [bw[nb % NB], g.b_pers], [bkm[nb % 4]])
        CP(S, "act" if nb % 2 == 0 else "dve", rowm[0:2, nb * 512:(nb + 1) * 512], pbk, [bkm[nb % 4]], [brow])
    for ch in range(96):
        TR(S, pm[:, ch * 2:ch * 2 + 2], rowm[0:2, ch * 128:(ch + 1) * 128], g.ident[0:2, 0:2],
           [brow, g.b_const], [bps])
    pm3 = pm.rearrange("p (c t) -> p c t", t=2)
    g.modx = g.pers.f32(96)
    g.modc = g.pers.f32(96)
    TT(S, "dve", g.modx, pm3[:, :, 0], modb, ALU.add, [bps, g.b_pers], [g.b_pers])
    TT(S, "dve", g.modc, pm3[:, :, 1], modb, ALU.add, [bps, g.b_pers], [g.b_pers])
    gm = g.pers.f32(16)
    gf = g.pers.f32(16)
    colvec(g, I["norm_mix_g"].rearrange("(k p) -> k p", p=128), 16, gm)
    colvec(g, I["norm_ffn_g"].rearrange("(k p) -> k p", p=128), 16, gf)
    g.A1x, g.A1c, g.A2x = g.pers.f32(16), g.pers.f32(16), g.pers.f32(16)
    for dst, mod, off, gain in ((g.A1x, g.modx, 16, gm), (g.A1c, g.modc, 16, gm), (g.A2x, g.modx, 64, gf)):
        STT(S, "dve", dst, mod[:, off:off + 16], 1.0, gain, ALU.add, ALU.mult, [g.b_pers], [g.b_pers])
    g.B1x, g.B1c, g.B2x = g.modx[:, 0:16], g.modc[:, 0:16], g.modx[:, 48:64]
    gsc = scratch(g, "gates", [2, 16, 128], F32)
    rows = g.pers.f32(256)
    for i, off in enumerate((32, 80)):
        TR(S, g.ps[0:16, 3584:3712], g.modx[:, off:off + 16], g.ident, [g.b_pers, g.b_const], [g.b_cvp])
        CP(S, "dve", rows[0:16, i * 128:(i + 1) * 128], g.ps[0:16, 3584:3712], [g.b_cvp], [g.b_pers])
        DMA(S, "sp", gsc[i], rows[0:16, i * 128:(i + 1) * 128], [g.b_pers], [g.sb["gates"]])


def stage_proj(g):
    S, nc, A = g.S, g.nc, g.arena
    A.reset()
    I = g.inp
    P = scratch(g, "P", [4096 + 0, T], F32)
    LW = scratch(g, "LOGW", [2, 1024, T], F32)
    AAs = scratch(g, "AA", [2, 1024, T], F32)
    GG = scratch(g, "GG", [1024, NOWN], F32)
    wup = [A.f32(1024) for _ in range(2)]
    aup = [A.f32(1024) for _ in range(2)]
    gup0, gup1 = A.f32(1024), A.f32(1024)
    bsm = Buf()
    for d in range(2):
        DMA(S, "sp", wup[d][0:64, :], I["rwkv_w_up"][d], [], [bsm])
        DMA(S, "sp", aup[d][0:64, :], I["rwkv_a_up"][d], [], [bsm])
    DMA(S, "sp", gup0, I["rwkv_g_up"][0:128, :], [], [bsm])
    DMA(S, "sp", gup1[0:32, :], I["rwkv_g_up"][128:160, :], [], [bsm])
    w0c = [g.pers.f32(8) for _ in range(2)]
    a0c = [g.pers.f32(8) for _ in range(2)]
    for d in range(2):
        colvec(g, I["rwkv_w0"][d].rearrange("(k p) -> k p", p=128), 8, w0c[d])
        colvec(g, I["rwkv_a0"][d].rearrange("(k p) -> k p", p=128), 8, a0c[d])
    NCH = len(CHUNKS)
    mu0, mu1, muc = g.pers.f32(NCH), g.pers.f32(NCH), g.pers.f32(NCH)
    S.op("dve", lambda e: e.memset(mu0, 0.0), [], [g.b_pers])
    S.op("dve", lambda e: e.memset(mu1, 0.0), [], [g.b_pers])
    colvec(g, I["shift_mu"][0, 0:3072].rearrange("(k p) -> k p", p=128), 24, mu0[:, 0:24])
    colvec(g, I["shift_mu"][1, 0:3072].rearrange("(k p) -> k p", p=128), 24, mu1[:, 0:24])
    for ci in range(24, 30):
        c0, M, _ = CHUNKS[ci]
        for mu, r in ((mu0, 0), (mu1, 1)):
            DMA(S, "sp", mu[0:M, ci:ci + 1], I["shift_mu"][r, c0:c0 + M].rearrange("(p o) -> p o", o=1),
                [], [g.b_pers], slow=True)
    TT(S, "dve", muc, mu0, mu1, ALU.add, [g.b_pers], [g.b_pers])
    TS(S, "dve", muc, muc, -1.0, 1.0, ALU.mult, ALU.add, [g.b_pers], [g.b_pers])

    xt = [A.f32(2048) for _ in range(2)]
    bxt = [Buf() for _ in range(2)]
    junk = A.bf16(2048)
    bjunk = Buf()
    xn = [A.bf16(2048) for _ in range(2)]
    bxn = [Buf() for _ in range(2)]
    st = A.f32(8)
    bst = Buf()
    hT = [A.bf16(16 * 512) for _ in range(2)]
    bhT = [Buf() for _ in range(2)]
    NWB = 3
    wf = [A.f32(16 * 128) for _ in range(NWB)]
    bwf = [Buf() for _ in range(NWB)]
    wb = [A.bf16(16 * 128) for _ in range(3)]
    bwb = [Buf() for _ in range(3)]
    WB = scratch(g, "WB", [len(CHUNKS), 128, 16 * 128], BF16)
    bWB = [Buf() for _ in range(len(CHUNKS))]
    ob = [A.f32(512) for _ in range(3)]
    bob = [Buf() for _ in range(3)]
    lora = {k: A.f32(512) for k in ("gd0", "gd1", "wd0", "wd1", "ad0", "ad1")}
    blora = Buf()
    ptr = g.ps[:, 0:1024].bitcast(BF16)
    bptr = Buf()
    pp = [g.ps[:, 1024 + 512 * i:1536 + 512 * i] for i in range(3)]
    bpp = [Buf() for _ in range(3)]
    pl = [g.ps[:, 2560 + 512 * i:3072 + 512 * i] for i in range(2)]
    bpl = [Buf() for _ in range(2)]
    wv = I["w_in"].rearrange("(kc p) n -> p kc n", p=128)
    nchunk = 0
    nl = 0
    cnt = {"nsub": 0}

    def prep_sub(ti, sub):
        t0, n = TILES[ti]
        nsub = cnt["nsub"]
        h3 = hT[ti % 2].rearrange("p (k t) -> p k t", t=512)
        A1, B1 = (g.A1c, g.B1c) if ti == 0 else (g.A1x, g.B1x)
        x_ = xt[nsub % 2]
        bx = bxt[nsub % 2]
        src = I["ctx"][sub * 128:(sub + 1) * 128, :] if ti == 0 else \
            I["x"][t0 - 256 + sub * 128:t0 - 256 + (sub + 1) * 128, :]
        DMA(S, "sp", x_, src, [], [bx])
        ACT(S, junk, x_, AF.Square, [bx], [bjunk, bst], accum_out=st[:, 0:1])
        TS(S, "dve", st[:, 1:2], st[:, 0:1], 1.0 / D, 1e-6, ALU.mult, ALU.add, [bst], [bst])
        ACT(S, st[:, 2:3], st[:, 1:2], AF.Sqrt, [bst], [bst])
        S.op("dve", lambda e, o=st[:, 3:4], i_=st[:, 2:3]: e.reciprocal(o, i_), [bst], [bst])
        xb = xn[nsub % 2]
        TS(S, "dve", xb, x_, st[:, 3:4], None, ALU.mult, None, [bx, bst], [bxn[nsub % 2]])
        for kc in range(16):
            TR(S, ptr[:, kc * 128:(kc + 1) * 128], xb[:, kc * 128:(kc + 1) * 128], g.identb,
               [bxn[nsub % 2], g.b_const], [bptr])
        for kc in range(16):
            dst = h3[:, kc, sub * 128:(sub + 1) * 128]
            srcp = ptr[:, kc * 128:(kc + 1) * 128]
            if kc % 2 == 0:
                TS(S, "dve", dst, srcp, A1[:, kc:kc + 1], B1[:, kc:kc + 1], ALU.mult, ALU.add,
                   [bptr, g.b_pers], [bhT[ti % 2]])
            else:
                ACT(S, dst, srcp, AF.Identity, [bptr, g.b_pers], [bhT[ti % 2]],
                    bias=B1[:, kc:kc + 1], scale=A1[:, kc:kc + 1])
        cnt["nsub"] = nsub + 1

    for sub in range(TILES[0][1] // 128):
        prep_sub(0, sub)
    for ti, (t0, n) in enumerate(TILES):
        h3 = hT[ti % 2].rearrange("p (k t) -> p k t", t=512)
        nxt = [(ti + 1, s_) for s_ in range(TILES[ti + 1][1] // 128)] if ti + 1 < len(TILES) else []
        rows = n // 64 if ti > 0 else 1
        rl = n // rows
        for ci, (c0, M, kind) in enumerate(CHUNKS):
            wb_ = wb[nchunk % 3]
            bwb_ = bwb[nchunk % 3]
            wb3 = wb_.rearrange("p (k m) -> p k m", m=128)
            qd = "sp" if nchunk % 2 == 0 else "pool"
            if ti == 0:
                w_ = wf[nchunk % NWB]
                w3 = w_.rearrange("p (k m) -> p k m", m=128)
                DMA(S, qd, w3[:, :, 0:M], wv[:, :, c0:c0 + M], [], [bwf[nchunk % NWB]])
                CP(S, "pool" if nchunk % 2 == 0 else "act", wb3[:, :, 0:M], w3[:, :, 0:M],
                   [bwf[nchunk % NWB]], [bwb_])
                DMA(S, "pool" if nchunk % 2 == 0 else "sp", WB[ci], wb_, [bwb_], [bWB[ci]])
            else:
                DMA(S, qd, wb_, WB[ci], [bWB[ci]], [bwb_])
            p_ = pp[nchunk % 3]
            bp = bpp[nchunk % 3]
            for kc in range(16):
                MM(S, p_[0:M, 0:n], wb3[:, kc, 0:M], h3[:, kc, 0:n], kc == 0, kc == 15,
                   [bwb_, bhT[ti % 2]], [bp])
            if kind in ("rkv", "u"):
                o_ = ob[nchunk % 3]
                bo = bob[nchunk % 3]
            else:
                o_ = lora[kind]
                bo = blora
            if kind == "u":
                CP(S, "act", o_[0:M, 0:n], p_[0:M, 0:n], [bp], [bo])
            else:
                ACT(S, o_[0:M, 0:n], p_[0:M, 0:n], AF.Copy, [bp, g.b_pers], [bo], scale=muc[0:M, ci:ci + 1])
                o3 = o_[0:M, 0:n].rearrange("p (r t) -> p r t", t=rl)
                p3 = p_[0:M, 0:n].rearrange("p (r t) -> p r t", t=rl)
                STT(S, "dve", o3[:, :, 1:rl], p3[:, :, 0:rl - 1], mu0[0:M, ci:ci + 1], o3[:, :, 1:rl],
                    ALU.mult, ALU.add, [bp, g.b_pers, bo], [bo])
                STT(S, "dve", o3[:, :, 0:rl - 1], p3[:, :, 1:rl], mu1[0:M, ci:ci + 1], o3[:, :, 0:rl - 1],
                    ALU.mult, ALU.add, [bp, g.b_pers, bo], [bo])
            if kind == "rkv":
                DMA(S, "pool", P[c0:c0 + M, t0:t0 + n], o_[0:M, 0:n], [bo], [g.sb["P"]])
            elif kind == "u":
                r0 = 3072 + (c0 - 3488)
                DMA(S, "pool", P[r0:r0 + M, t0:t0 + n], o_[0:M, 0:n], [bo], [g.sb["P"]])
            elif kind in ("gd0", "gd1"):
                ACT(S, o_[0:M, 0:n], o_[0:M, 0:n], AF.Sigmoid, [bo], [bo])
            elif kind in ("wd0", "wd1"):
                ACT(S, o_[0:M, 0:n], o_[0:M, 0:n], AF.Tanh, [bo], [bo])
            nchunk += 1
            if nxt and ci % 8 == 3:
                prep_sub(*nxt.pop(0))
        while nxt:
            prep_sub(*nxt.pop(0))
        for d in range(2):
            for hc in range(8):
                for which in range(2):
                    pq = pl[nl % 2]
                    bq = bpl[nl % 2]
                    o_ = ob[nl % 3]
                    bo = bob[nl % 3]
                    if which == 0:
                        MM(S, pq[:, 0:n], wup[d][0:64, hc * 128:(hc + 1) * 128],
                           lora["wd%d" % d][0:64, 0:n], True, True, [bsm, blora], [bq])
                        ACT(S, o_[:, 0:n], pq[:, 0:n], AF.Sigmoid, [bq, g.b_pers], [bo], bias=w0c[d][:, hc:hc + 1])
                        TS(S, "dve", o_[:, 0:n], o_[:, 0:n], -0.6065306597126334, None, ALU.mult, None, [bo], [bo])
                        DMA(S, "pool", LW[d, hc * 128:(hc + 1) * 128, t0:t0 + n], o_[:, 0:n], [bo], [g.sb["LOGW"]])
                    else:
                        MM(S, pq[:, 0:n], aup[d][0:64, hc * 128:(hc + 1) * 128],
                           lora["ad%d" % d][0:64, 0:n], True, True, [bsm, blora], [bq])
                        ACT(S, o_[:, 0:n], pq[:, 0:n], AF.Sigmoid, [bq, g.b_pers], [bo], bias=a0c[d][:, hc:hc + 1])
                        DMA(S, "pool", AAs[d, hc * 128:(hc + 1) * 128, t0:t0 + n], o_[:, 0:n], [bo], [g.sb["AA"]])
                    nl += 1
        if OWN0 <= t0 < OWN1:
            for hc in range(8):
                pq = pl[nl % 2]
                bq = bpl[nl % 2]
                o_ = ob[nl % 3]
                bo = bob[nl % 3]
                MM(S, pq[:, 0:n], gup0[:, hc * 128:(hc + 1) * 128], lora["gd0"][:, 0:n],
                   True, False, [bsm, blora], [bq])
                MM(S, pq[:, 0:n], gup1[0:32, hc * 128:(hc + 1) * 128],
                   lora["gd1"][0:32, 0:n], False, True, [bsm, blora], [bq])
                CP(S, "act", o_[:, 0:n], pq[:, 0:n], [bq], [bo])
                DMA(S, "pool", GG[hc * 128:(hc + 1) * 128, t0 - OWN0:t0 - OWN0 + n], o_[:, 0:n], [bo], [g.sb["GG"]])
                nl += 1


def build(stop_after=99, dbg=()):
    nc = bass.Bass("TRN2", target_bir_lowering=False)
    g = G()
    g.nc = nc
    g.dbg = set(dbg)
    g.scr = {}
    g.sb = {}
    g.S = Sched(nc)
    I = {}

    def inp(name, shape):
        I[name] = nc.dram_tensor(name, list(shape), F32, kind="ExternalInput").ap()

    inp("x", [4096, D]); inp("ctx", [256, D]); inp("c", [D]); inp("c_ctx", [D])
    inp("mod_w", [D, 6 * D]); inp("mod_b", [6 * D]); inp("norm_mix_g", [D]); inp("w_in", [D, NIN])
    inp("shift_mu", [2, 3488]); inp("rwkv_w0", [2, 1024]); inp("rwkv_w_up", [2, 64, 1024])
    inp("rwkv_a0", [2, 1024]); inp("rwkv_a_up", [2, 64, 1024]); inp("rwkv_g_up", [160, 1024])
    inp("rwkv_k_k", [1024]); inp("rwkv_k_a", [1024]); inp("rwkv_r_k", [1024]); inp("lnx_w", [1024])
    inp("lnx_b", [1024]); inp("s5q_are", [128, 64]); inp("s5q_aim", [128, 64]); inp("s5q_lst", [128, 64])
    inp("s5q_bre", [128, 1024]); inp("s5q_bim", [128, 1024]); inp("s5_c_re", [2, 64, 16, 64])
    inp("s5_c_im", [2, 64, 16, 64]); inp("s5_d", [1024]); inp("s5_glu_w", [1024, 1024]); inp("s5_glu_b", [1024])
    inp("w_out", [D, D]); inp("norm_ffn_g", [D]); inp("ffn_w_up", [D, 2 * DFF]); inp("ffn_conv_w", [3, DFF])
    inp("ffn_conv_b", [DFF]); inp("ffn_w_down", [DFF, D]); inp("final_norm_g", [D])
    inp("consts", [128, NCONST])
    g.inp = I
    g.out = nc.dram_tensor("out", [NOWN, D], F32, kind="ExternalOutput").ap()

    AW = 43008
    arena_t = nc.alloc_sbuf_tensor("arena", [128, AW], F32)
    pers_t = nc.alloc_sbuf_tensor("pers", [128, 4096], F32)
    g.arena = Arena(arena_t[:, :], AW, g.S)
    g.pers = Arena(pers_t[:, :], 4096)
    g.ps = nc.alloc_psum_tensor("ps", [128, 4096], F32)[:, :]
    g.b_pers = Buf("pers")
    g.b_const = Buf("const")
    g.b_cvs = Buf()
    g.b_cvp = Buf()
    S = g.S
    cst = g.pers.f32(NCONST)
    DMA(S, "sp", cst, I["consts"], [], [g.b_const])
    g.cst = cst
    g.ident = cst[:, 0:128]
    g.cv_stage = g.pers.f32(128)
    idb = g.pers.bf16(128)
    CP(S, "dve", idb, g.ident, [g.b_const], [g.b_const])
    g.identb = idb

    stage_mod(g)
    if stop_after >= 1:
        stage_proj(g)
    if stop_after >= 2 and not SKIP_RWKV:
        stage_rwkv(g)
    if stop_after >= 3:
        stage_s5(g)
    if stop_after >= 4:
        stage_tail(g)
    if DEBUG and "modx" in g.dbg:
        mo = nc.dram_tensor("modx_o", [128, 96], F32, kind="ExternalOutput").ap()
        DMA(S, "sp", mo, g.modx, [g.b_pers], [Buf()])
    if stop_after < 99:
        z = g.pers.f32(8)
        S.op("dve", lambda e: e.memset(z, 0.0), [], [g.b_pers])
        DMA(S, "sp", g.out[0:128, 0:8], z, [g.b_pers], [Buf()])
    S.finish()
    return nc


NCONST = 968


def make_consts():
    c = np.zeros((128, NCONST), np.float32)
    c[:, 0:128] = np.eye(128, dtype=np.float32)
    i = np.arange(64)
    for d, off in ((0, 128), (1, 448)):
        if d == 0:
            ms = (i[None, :] > i[:, None]); mi = (i[None, :] >= i[:, None])
        else:
            ms = (i[None, :] < i[:, None]); mi = (i[None, :] <= i[:, None])
        m5 = np.concatenate([ms, mi, ms, mi, ms.T], axis=1).astype(np.float32)
        c[0:64, off:off + 320] = m5
        c[64:128, off:off + 320] = m5
    c[:, 768:832] = 1.0
    c[:, 832:896] = 1.0 / 64.0
    p = np.arange(128)
    c[:, 896:904] = (p[:, None] // 16 == np.arange(8)[None, :]).astype(np.float32)
    c[0:64, 904:968] = np.eye(64, dtype=np.float32)
    c[64:128, 904:968] = np.eye(64, dtype=np.float32)
    return c


def core_inputs(inputs, b, j):
    f = lambda a: np.ascontiguousarray(np.asarray(a, dtype=np.float32))
    rev = (j == 1)
    sw = (lambda a: a[::-1]) if rev else (lambda a: a)
    m = {}
    x = inputs["x"][b]
    ctx = inputs["ctx"][b]
    m["x"] = f(x[::-1] if rev else x)
    m["ctx"] = f(ctx[::-1] if rev else ctx)
    m["c"] = f(inputs["c"][b])
    m["c_ctx"] = f(inputs["c_ctx"])
    m["mod_w"] = f(inputs["mod_w"][0]); m["mod_b"] = f(inputs["mod_b"][0])
    m["norm_mix_g"] = f(inputs["norm_mix_g"][0])
    w_in = np.asarray(inputs["w_in"][0])
    mu = np.asarray(inputs["shift_mu"][0])
    if rev:
        perm = np.arange(NIN)
        perm[3232:3296], perm[3296:3360] = np.arange(3296, 3360), np.arange(3232, 3296)
        perm[3360:3424], perm[3424:3488] = np.arange(3424, 3488), np.arange(3360, 3424)
        w_in = w_in[:, perm]
        mu = mu[::-1][:, perm[:3488]]
    m["w_in"] = f(w_in)
    m["shift_mu"] = f(mu)
    for k in ("rwkv_w0", "rwkv_w_up", "rwkv_a0", "rwkv_a_up", "s5_c_re", "s5_c_im"):
        m[k] = f(sw(np.asarray(inputs[k][0])))
    for k, nm in (("s5_a_re", "s5q_are"), ("s5_a_im", "s5q_aim")):
        a_ = sw(np.asarray(inputs[k][0])).reshape(2, 8, 8, 4, 16)
        m[nm] = f(a_.transpose(2, 4, 1, 0, 3).reshape(128, 64))
    ls = sw(np.asarray(inputs["s5_log_step"][0])).reshape(2, 8, 8)
    ls = np.broadcast_to(ls.transpose(2, 1, 0)[:, None, :, :, None], (8, 16, 8, 2, 4))
    m["s5q_lst"] = f(ls.reshape(128, 64))
    for k, nm in (("s5_b_re", "s5q_bre"), ("s5_b_im", "s5q_bim")):
        b_ = sw(np.asarray(inputs[k][0])).reshape(2, 8, 8, 4, 16, 16)
        m[nm] = f(b_.transpose(2, 4, 1, 0, 3, 5).reshape(128, 1024))
    for k in ("rwkv_g_up", "rwkv_k_k", "rwkv_k_a", "lnx_w", "lnx_b", "s5_d", "s5_glu_w", "s5_glu_b", "w_out",
              "norm_ffn_g", "ffn_w_up", "ffn_conv_b", "ffn_w_down"):
        m[k] = f(inputs[k][0])
    m["rwkv_r_k"] = f(np.asarray(inputs["rwkv_r_k"][0]).reshape(1024))
    cw = np.asarray(inputs["ffn_conv_w"][0])
    m["ffn_conv_w"] = f(cw[::-1] if rev else cw)
    m["final_norm_g"] = f(inputs["final_norm_g"])
    m["consts"] = make_consts()
    return m


def kernel(**inputs):
    nc = build()
    in_maps = [core_inputs(inputs, c // 2, c % 2) for c in range(8)]
    res = run_bass_kernel_spmd(nc, in_maps, core_ids=list(range(8)))
    out = np.zeros((4, 4096, D), np.float32)
    for c in range(8):
        b, j = c // 2, c % 2
        o = np.asarray(res.results[c]["out"])
        if j == 0:
            out[b, 0:2048] = o
        else:
            out[b, 2048:4096] = o[::-1]
    return out


def colvec_w(g, dram2d, rows, width, dst):
    S = g.S
    st = g.cv_stage
    DMA(S, "sp", st[0:rows, 0:width], dram2d, [], [g.b_cvs])
    TR(S, g.ps[0:width, 3584:3584 + rows], st[0:rows, 0:width], g.ident[0:rows, 0:rows],
       [g.b_cvs, g.b_const], [g.b_cvp])
    CP(S, "dve", dst, g.ps[0:width, 3584:3584 + rows], [g.b_cvp], [g.b_pers])


def stage_rwkv(g):
    S, nc, A = g.S, g.nc, g.arena
    A.reset()
    I = g.inp
    P, LW, AAs, GG = g.scr["P"], g.scr["LOGW"], g.scr["AA"], g.scr["GG"]
    OT = scratch(g, "OT", [2048, NOWN], BF16)
    LN = (slice(0, 64), slice(64, 128))
    ones = g.cst[:, 768:832]
    ones64 = g.cst[:, 832:896]
    id2 = g.cst[:, 904:968]
    M5 = [g.cst[:, 128:448], g.cst[:, 448:768]]
    kkc, kac, omka, rkc, lwc, lbc = (g.pers.f32(8) for _ in range(6))
    for dst, nm in ((kkc, "rwkv_k_k"), (kac, "rwkv_k_a"), (rkc, "rwkv_r_k"), (lwc, "lnx_w"), (lbc, "lnx_b")):
        colvec(g, I[nm].rearrange("(h p) -> h p", p=128), 8, dst)
    TS(S, "dve", omka, kac, -1.0, 1.0, ALU.mult, ALU.add, [g.b_pers], [g.b_pers])
    bk = [Buf("bank%d" % i) for i in range(8)]
    bank = [g.ps[:, 512 * i:512 * (i + 1)] for i in range(8)]

    def MM2(out, lhsT, rhs, start, stop, reads, writes):
        for ln in LN:
            MM(S, out[ln], lhsT[ln], rhs[ln], start, stop, reads, writes)

    id2b = g.pers.bf16(64)
    CP(S, "dve", id2b, id2, [g.b_const], [g.b_const])

    def TR2(out, in_, reads, writes, lowp=False):
        idm = id2b if (lowp and RWKV_BF16) else id2
        for ln in LN:
            MM(S, out[ln], in_[ln], idm[ln], True, True, reads, writes)

    def t64(n=512, nb=1):
        return [A.f32(n) for _ in range(nb)]

    YH = t64(NOWN, 2)
    bYH = [Buf(), Buf()]
    Rt, Kt, Vt, At, Lt = t64(512, 2), t64(512, 2), t64(512, 2), t64(512, 2), t64(512, 2)
    bin_ = [Buf(), Buf()]
    kk, sq, rin, kap = t64()[0], t64()[0], t64()[0], t64()[0]
    bprep = Buf()
    tt_, kd, bb, tmp, E1, E2, E3, E4 = (t64()[0] for _ in range(8))
    lp = A.bf16 if RWKV_BF16 else A.f32
    AR, BK = lp(1024), lp(1024)
    BH, KH = lp(512), lp(512)
    bop = Buf()
    LWtm, BHtm, KHtm, Vtm = t64()[0], lp(512), lp(512), lp(512)
    btm = Buf()
    X = lp(1024)
    bX = Buf()
    A5s = lp(8 * 320)
    bA5 = Buf()
    PPb = [lp(512), lp(512)]
    bPP = [Buf(), Buf()]
    QT, YL, PHI, ZZ = t64(512, 2), t64(512, 2), t64(512, 2), t64(512, 2)
    bpost = [Buf(), Buf()]
    Sst = t64(64, 2)
    bS = Buf()
    yh, cen, rstd, rk, Gt = t64()[0], t64()[0], t64()[0], t64()[0], t64()[0]
    ob = A.bf16(512)
    bout = Buf()
    nio = 0
    npost = 0
    for hp in range(8):
        r0, k0, v0 = hp * 128, 1024 + hp * 128, 2048 + hp * 128
        hc = slice(hp, hp + 1)
        for d in range(2):
            order = [0, 1, 2, 3, 4] if d == 0 else [0, 8, 7, 6, 5, 4, 3, 2, 1]
            S.op("dve", lambda e, s_=Sst[0]: e.memset(s_, 0.0), [], [bS])
            scur = 0
            for ti in order:
                t0, n = TILES[ti]
                nch = n // 64
                own = OWN0 <= t0 < OWN1
                ib = nio % 2
                nio += 1
                R, Kx, V, AAx, L = Rt[ib], Kt[ib], Vt[ib], At[ib], Lt[ib]
                for qi_, (dst, src) in enumerate(((R, P[r0:r0 + 128, t0:t0 + n]), (Kx, P[k0:k0 + 128, t0:t0 + n]),
                                                  (V, P[v0:v0 + 128, t0:t0 + n]), (AAx, AAs[d, r0:r0 + 128, t0:t0 + n]),
                                                  (L, LW[d, r0:r0 + 128, t0:t0 + n]))):
                    DMA(S, "sp" if qi_ % 2 == 0 else "pool", dst[:, 0:n], src,
                        [g.sb["P"], g.sb["AA"], g.sb["LOGW"]], [bin_[ib]])
                bi = bin_[ib]
                sl = slice(0, n)
                TS(S, "dve", kk[:, sl], Kx[:, sl], kkc[:, hc], None, ALU.mult, None, [bi, g.b_pers], [bprep])
                ACT(S, sq[:, sl], kk[:, sl], AF.Square, [bprep], [bprep])
                MM2(bank[0][:, sl], ones, sq[:, sl], True, True, [bprep, g.b_const], [bk[0]])
                ACT(S, rin[:, sl], bank[0][:, sl], AF.Sqrt, [bk[0]], [bprep])
                TS(S, "dve", rin[:, sl], rin[:, sl], 1e-12, None, ALU.max, None, [bprep], [bprep])
                S.op("dve", lambda e, o=rin[:, sl]: e.reciprocal(o, o), [bprep], [bprep])
                TT(S, "dve", kap[:, sl], kk[:, sl], rin[:, sl], ALU.mult, [bprep], [bprep])
                TS(S, "dve", tt_[:, sl], AAx[:, sl], kac[:, hc], omka[:, hc], ALU.mult, ALU.add,
                   [bi, g.b_pers], [bop])
                TT(S, "pool", kd[:, sl], Kx[:, sl], tt_[:, sl], ALU.mult, [bi, bop], [bop])
                TT(S, "pool", bb[:, sl], kap[:, sl], AAx[:, sl], ALU.mult, [bi, bprep, bop], [bop])
                for c in range(nch):
                    TR2(bank[1][:, c * 64:(c + 1) * 64], L[:, c * 64:(c + 1) * 64], [bi, g.b_const], [bk[1]])
                CP(S, "act", LWtm[:, sl], bank[1][:, sl], [bk[1]], [btm])
                for c in range(nch):
                    MM2(bank[0][:, c * 64:(c + 1) * 64], LWtm[:, c * 64:(c + 1) * 64], M5[d][:, 64:128],
                        True, True, [btm, g.b_const], [bk[0]])
                Gp = bank[0][:, sl]
                ACT(S, E1[:, sl], Gp, AF.Exp, [bk[0]], [bop])
                ACT(S, E3[:, sl], Gp, AF.Exp, [bk[0]], [bop], scale=-1.0)
                TT(S, "dve", tmp[:, sl], Gp, L[:, sl], ALU.subtract, [bk[0], bi], [bop])
                ACT(S, E2[:, sl], tmp[:, sl], AF.Exp, [bop], [bop])
                e13 = E1[:, sl].rearrange("p (c t) -> p c t", t=64)
                gcv = e13[:, :, 63] if d == 0 else e13[:, :, 0]
                e33 = E3[:, sl].rearrange("p (c t) -> p c t", t=64)
                e43 = E4[:, sl].rearrange("p (c t) -> p c t", t=64)
                TT(S, "dve", e43, e33, gcv.unsqueeze(2).broadcast_to([128, nch, 64]), ALU.mult, [bop], [bop])
                AR4 = AR[:, 0:2 * n].rearrange("p (c w t) -> p c w t", w=2, t=64)
                BK4 = BK[:, 0:2 * n].rearrange("p (c w t) -> p c w t", w=2, t=64)
                v3 = lambda a: a[:, sl].rearrange("p (c t) -> p c t", t=64)
                STT(S, "dve", AR4[:, :, 0, :], v3(kap), -1.0, v3(E2), ALU.mult, ALU.mult, [bprep, bop], [bop])
                TT(S, "pool", AR4[:, :, 1, :], v3(R), v3(E1), ALU.mult, [bi, bop], [bop])
                TT(S, "pool", BK4[:, :, 0, :], v3(bb), v3(E3), ALU.mult, [bop], [bop])
                TT(S, "dve", BK4[:, :, 1, :], v3(kd), v3(E3), ALU.mult, [bop], [bop])
                TT(S, "pool", BH[:, sl], bb[:, sl], E4[:, sl], ALU.mult, [bop], [bop])
                TT(S, "dve", KH[:, sl], kd[:, sl], E4[:, sl], ALU.mult, [bop], [bop])
                X3 = X[:, 0:2 * n].rearrange("p (c w) -> p c w", w=128)
                for c in range(nch):
                    TR2(bank[1][:, c * 64:(c + 1) * 64], AR4[:, c, 0, :], [bop, g.b_const], [bk[1]], lowp=True)
                CP(S, "act", X3[:, :, 0:64], bank[1][:, sl].rearrange("p (c t) -> p c t", t=64), [bk[1]], [bX])
                for src_, dst_ in ((BH, BHtm), (KH, KHtm), (V, Vtm)):
                    for c in range(nch):
                        TR2(bank[1][:, c * 64:(c + 1) * 64], src_[:, c * 64:(c + 1) * 64], [bop, bi, g.b_const], [bk[1]],
                            lowp=src_ is not V)
                    CP(S, "act" if dst_ is not KHtm else "dve", dst_[:, sl], bank[1][:, sl], [bk[1]], [btm])
                A53 = A5s[:, 0:nch * 320].rearrange("p (c w) -> p c w", w=320)
                for c in range(nch):
                    pb = bank[2 + c % 2]
                    bpb = bk[2 + c % 2]
                    arc = AR[:, c * 128:(c + 1) * 128]
                    MM2(pb[:, 0:128], BK4[:, c, 0, :], arc, True, True, [bop], [bpb])
                    MM2(pb[:, 128:256], BK4[:, c, 1, :], arc, True, True, [bop], [bpb])
                    MM2(pb[:, 256:320], AR4[:, c, 0, :], BK4[:, c, 0, :], True, True, [bop], [bpb])
                    TT(S, "dve", A53[:, c, :], pb[:, 0:320], M5[d], ALU.mult, [bpb, g.b_const], [bA5])
                for c in range(nch):
                    MM2(bank[4][:, c * 64:(c + 1) * 64], A53[:, c, 128:192], Vtm[:, c * 64:(c + 1) * 64],
                        True, True, [bA5, btm], [bk[4]])
                CP(S, "act", X3[:, :, 64:128], bank[4][:, sl].rearrange("p (c t) -> p c t", t=64), [bk[4]], [bX])
                for g0 in range(0, nch, 4):
                    gn = min(4, nch - g0)
                    for lvl in range(6):
                        for c in range(g0, g0 + gn):
                            if lvl == 0:
                                Pm, PTm = A53[:, c, 256:320], A53[:, c, 0:64]
                                rd = [bA5]
                            else:
                                pp_ = PPb[(lvl - 1) % 2]
                                Pm = pp_[:, (c - g0) * 128:(c - g0) * 128 + 64]
                                PTm = pp_[:, (c - g0) * 128 + 64:(c - g0) * 128 + 128]
                                rd = [bPP[(lvl - 1) % 2]]
                            MM2(bank[4][:, (c - g0) * 128:(c - g0 + 1) * 128], PTm, X3[:, c, :], True, True,
                                rd + [bX], [bk[4]])
                            if lvl < 5:
                                MM2(bank[5][:, (c - g0) * 128:(c - g0) * 128 + 64], PTm, Pm, True, True, rd, [bk[5]])
                                MM2(bank[5][:, (c - g0) * 128 + 64:(c - g0 + 1) * 128], Pm, PTm, True, True, rd,
                                    [bk[5]])
                        xs = X[:, g0 * 128:(g0 + gn) * 128]
                        TT(S, "dve", xs, xs, bank[4][:, 0:gn * 128], ALU.add, [bk[4], bX], [bX])
                        if lvl < 5:
                            CP(S, "act", PPb[lvl % 2][:, 0:gn * 128], bank[5][:, 0:gn * 128], [bk[5]], [bPP[lvl % 2]])
                pbi = npost % 2
                npost += 1
                bpo = bpost[pbi]
                e1_or = AR4[:, :, 1, :]
                for c in range(nch):
                    W1, W2 = X3[:, c, 0:64], X3[:, c, 64:128]
                    cs = slice(c * 64, (c + 1) * 64)
                    if own:
                        MM2(bank[2][:, cs], W1, A53[:, c, 64:128], True, True, [bX, bA5], [bk[2]])
                        MM2(bank[3][:, cs], W2, A53[:, c, 64:128], True, False, [bX, bA5], [bk[3]])
                        MM2(bank[3][:, cs], Vtm[:, cs], A53[:, c, 192:256], False, True, [btm, bA5], [bk[3]])
                    MM2(bank[6][:, cs], W1, BHtm[:, cs], True, True, [bX, btm], [bk[6]])
                    MM2(bank[7][:, cs], BHtm[:, cs], W2, True, False, [bX, btm], [bk[7]])
                    MM2(bank[7][:, cs], KHtm[:, cs], Vtm[:, cs], False, True, [btm], [bk[7]])
                if own:
                    TT(S, "dve", QT[pbi][:, sl].rearrange("p (c t) -> p c t", t=64),
                       bank[2][:, sl].rearrange("p (c t) -> p c t", t=64), e1_or, ALU.add, [bk[2], bop], [bpo])
                    CP(S, "act", YL[pbi][:, sl], bank[3][:, sl], [bk[3]], [bpo])
                ph3 = PHI[pbi][:, sl].rearrange("p (c t) -> p c t", t=64)
                TT(S, "pool", ph3, id2.unsqueeze(1).broadcast_to([128, nch, 64]),
                   gcv.unsqueeze(2).broadcast_to([128, nch, 64]), ALU.mult, [g.b_const, bop], [bpo])
                TT(S, "dve", PHI[pbi][:, sl], PHI[pbi][:, sl], bank[6][:, sl], ALU.add, [bk[6], bpo], [bpo])
                CP(S, "act", ZZ[pbi][:, sl], bank[7][:, sl], [bk[7]], [bpo])
                corder = range(nch) if d == 0 else range(nch - 1, -1, -1)
                for c in corder:
                    cs = slice(c * 64, (c + 1) * 64)
                    Sc, Sn = Sst[scur], Sst[1 - scur]
                    if own:
                        yo = t0 - OWN0 + c * 64
                        MM2(bank[1][:, 0:64], Sc, QT[pbi][:, cs], True, True, [bS, bpo], [bk[1]])
                        TT(S, "dve", YH[d][:, yo:yo + 64], bank[1][:, 0:64], YL[pbi][:, cs], ALU.add,
                           [bk[1], bpo], [bYH[d]])
                    MM2(bank[1][:, 64:128], PHI[pbi][:, cs], Sc, True, True, [bS, bpo], [bk[1]])
                    TT(S, "dve", Sn, bank[1][:, 64:128], ZZ[pbi][:, cs], ALU.add, [bk[1], bpo], [bS])
                    scur = 1 - scur
        for ti in range(1, 5):
            t0, n = TILES[ti]
            o0 = t0 - OWN0
            ib = nio % 2
            nio += 1
            R, Kx, V = Rt[ib], Kt[ib], Vt[ib]
            for dst, src in ((R, P[r0:r0 + 128, t0:t0 + n]), (Kx, P[k0:k0 + 128, t0:t0 + n]),
                             (V, P[v0:v0 + 128, t0:t0 + n])):
                DMA(S, "sp", dst[:, 0:n], src, [g.sb["P"]], [bin_[ib]])
            DMA(S, "sp", Gt[:, 0:n], GG[r0:r0 + 128, o0:o0 + n], [g.sb["GG"]], [bout])
            bi = bin_[ib]
            TT(S, "pool", yh, YH[0][:, o0:o0 + n], YH[1][:, o0:o0 + n], ALU.add, [bYH[0], bYH[1]], [bout])
            MM2(bank[0], ones64, yh, True, True, [bout, g.b_const], [bk[0]])
            TT(S, "dve", cen, yh, bank[0], ALU.subtract, [bk[0], bout], [bout])
            ACT(S, yh, cen, AF.Square, [bout], [bout])
            MM2(bank[0], ones64, yh, True, True, [bout, g.b_const], [bk[0]])
            TS(S, "dve", rstd, bank[0], 64e-5, None, ALU.add, None, [bk[0]], [bout])
            ACT(S, rstd, rstd, AF.Sqrt, [bout], [bout])
            S.op("dve", lambda e, o=rstd: e.reciprocal(o, o), [bout], [bout])
            TT(S, "dve", cen, cen, rstd, ALU.mult, [bout], [bout])
            TS(S, "dve", cen, cen, lwc[:, hc], lbc[:, hc], ALU.mult, ALU.add, [bout, g.b_pers], [bout])
            STT(S, "dve", rk, R, rkc[:, hc], Kx, ALU.mult, ALU.mult, [bi, g.b_pers], [bout])
            MM2(bank[0], ones, rk, True, True, [bout, g.b_const], [bk[0]])
            TT(S, "dve", rk, bank[0], V, ALU.mult, [bk[0], bi], [bout])
            TT(S, "pool", cen, cen, rk, ALU.add, [bout], [bout])
            TT(S, "dve", ob, cen, Gt, ALU.mult, [bout], [bout])
            DMA(S, "pool", OT[r0:r0 + 128, o0:o0 + n], ob, [bout], [g.sb["OT"]])


def stage_s5(g):
    S, nc, A = g.S, g.nc, g.arena
    A.reset()
    I = g.inp
    P = g.scr["P"]
    ZT = scratch(g, "ZT", [1024, NOWN], F32)
    ident = g.ident
    bdm = g.cst[:, 896:904]
    MB = S5_MB
    NCTX, JB0, JB1 = 256 // MB, OWN0 // MB, OWN1 // MB
    bk = [Buf("bank%d" % i) for i in range(8)]
    bank = [g.ps[:, 512 * i:512 * (i + 1)] for i in range(8)]
    bpar = Buf("s5par")
    NQ = 64

    def tl(n=NQ):
        return A.f32(n)

    are, aim, lst = tl(), tl(), tl()
    v4 = lambda a: a.rearrange("p (gs d pb) -> p gs d pb", gs=8, d=2, pb=4)
    DMA(S, "sp", are, I["s5q_are"], [], [bpar])
    DMA(S, "sp", aim, I["s5q_aim"], [], [bpar])
    DMA(S, "sp", lst, I["s5q_lst"], [], [bpar])
    dt_, mag, ang, cc, ss, t1, t2, t3 = (tl() for _ in range(8))
    ACT(S, dt_, lst, AF.Exp, [bpar], [bpar])
    TT(S, "dve", mag, are, dt_, ALU.mult, [bpar], [bpar])
    ACT(S, mag, mag, AF.Exp, [bpar], [bpar])
    TT(S, "dve", ang, aim, dt_, ALU.mult, [bpar], [bpar])
    hp = g.pers.f32(8)
    S.op("dve", lambda e: e.memset(hp, 1.5707963267948966), [], [g.b_pers])
    ACT(S, ss, ang, AF.Sin, [bpar], [bpar], scale=1.0 / 16)
    ACT(S, cc, ang, AF.Sin, [bpar, g.b_pers], [bpar], scale=1.0 / 16, bias=hp[:, 0:1])

    def csq(cr, ci):
        TT(S, "dve", t1, cr, cr, ALU.mult, [bpar], [bpar])
        TT(S, "dve", t2, ci, ci, ALU.mult, [bpar], [bpar])
        TT(S, "dve", t3, cr, ci, ALU.mult, [bpar], [bpar])
        TT(S, "dve", cr, t1, t2, ALU.subtract, [bpar], [bpar])
        TS(S, "dve", ci, t3, 2.0, None, ALU.mult, None, [bpar], [bpar])

    for _ in range(4):
        csq(cc, ss)
    lr, li, nli = tl(), tl(), tl()
    TT(S, "dve", lr, mag, cc, ALU.mult, [bpar], [bpar])
    TT(S, "dve", li, mag, ss, ALU.mult, [bpar], [bpar])
    TS(S, "dve", nli, li, -1.0, None, ALU.mult, None, [bpar], [bpar])
    fr, fi, den, nr = tl(), tl(), tl(), tl()
    TT(S, "dve", t1, are, are, ALU.mult, [bpar], [bpar])
    TT(S, "dve", t2, aim, aim, ALU.mult, [bpar], [bpar])
    TT(S, "dve", den, t1, t2, ALU.add, [bpar], [bpar])
    S.op("dve", lambda e: e.reciprocal(den, den), [bpar], [bpar])
    TS(S, "dve", nr, lr, -1.0, None, ALU.add, None, [bpar], [bpar])
    TT(S, "dve", t1, nr, are, ALU.mult, [bpar], [bpar])
    TT(S, "dve", t2, li, aim, ALU.mult, [bpar], [bpar])
    TT(S, "dve", t1, t1, t2, ALU.add, [bpar], [bpar])
    TT(S, "dve", fr, t1, den, ALU.mult, [bpar], [bpar])
    TT(S, "dve", t1, li, are, ALU.mult, [bpar], [bpar])
    TT(S, "dve", t2, nr, aim, ALU.mult, [bpar], [bpar])
    TT(S, "dve", t1, t1, t2, ALU.subtract, [bpar], [bpar])
    TT(S, "dve", fi, t1, den, ALU.mult, [bpar], [bpar])
    TBr, TBi, nTBi = A.f32(MB * NQ), A.f32(MB * NQ), A.f32(MB * NQ)
    tb = lambda a, i: a[:, i * NQ:(i + 1) * NQ]
    CP(S, "dve", tb(TBr, 0), lr, [bpar], [bpar])
    CP(S, "dve", tb(TBi, 0), li, [bpar], [bpar])
    for i in range(1, MB):
        TT(S, "dve", t1, tb(TBr, i - 1), lr, ALU.mult, [bpar], [bpar])
        TT(S, "dve", t2, tb(TBi, i - 1), li, ALU.mult, [bpar], [bpar])
        TT(S, "dve", tb(TBr, i), t1, t2, ALU.subtract, [bpar], [bpar])
        TT(S, "dve", t1, tb(TBr, i - 1), li, ALU.mult, [bpar], [bpar])
        TT(S, "dve", t2, tb(TBi, i - 1), lr, ALU.mult, [bpar], [bpar])
        TT(S, "dve", tb(TBi, i), t1, t2, ALU.add, [bpar], [bpar])
    TS(S, "dve", nTBi, TBi, -1.0, None, ALU.mult, None, [bpar], [bpar])
    NL = 10
    LMr, LMi, nLMi = A.f32(NL * NQ), A.f32(NL * NQ), A.f32(NL * NQ)
    CP(S, "dve", tb(LMr, 0), tb(TBr, MB - 1), [bpar], [bpar])
    CP(S, "dve", tb(LMi, 0), tb(TBi, MB - 1), [bpar], [bpar])
    for k in range(1, NL):
        CP(S, "dve", tb(LMr, k), tb(LMr, k - 1), [bpar], [bpar])
        CP(S, "dve", tb(LMi, k), tb(LMi, k - 1), [bpar], [bpar])
        csq(tb(LMr, k), tb(LMi, k))
    TS(S, "dve", nLMi, LMi, -1.0, None, ALU.mult, None, [bpar], [bpar])
    Bq = [A.f32(NQ * 16) for _ in range(2)]
    DMA(S, "sp", Bq[0], I["s5q_bre"], [], [bpar])
    DMA(S, "sp", Bq[1], I["s5q_bim"], [], [bpar])
    Bbr, Bbi, tq = A.f32(NQ * 16), A.f32(NQ * 16), A.f32(NQ * 16)
    q3 = lambda a: a.rearrange("p (c h) -> p c h", h=16)
    bc = lambda a: a.unsqueeze(2).broadcast_to([128, NQ, 16])
    TT(S, "dve", q3(Bbr), q3(Bq[0]), bc(fr), ALU.mult, [bpar], [bpar])
    TT(S, "dve", q3(tq), q3(Bq[1]), bc(fi), ALU.mult, [bpar], [bpar])
    TT(S, "dve", Bbr, Bbr, tq, ALU.subtract, [bpar], [bpar])
    TT(S, "dve", q3(Bbi), q3(Bq[1]), bc(fr), ALU.mult, [bpar], [bpar])
    TT(S, "dve", q3(tq), q3(Bq[0]), bc(fi), ALU.mult, [bpar], [bpar])
    TT(S, "dve", Bbi, Bbi, tq, ALU.add, [bpar], [bpar])
    sdc = g.pers.f32(8)
    colvec(g, I["s5_d"].rearrange("(k p) -> k p", p=128), 8, sdc)

    u = A.f32(T)
    bu_ = Buf()
    sre2, sim2 = [A.f32(T), A.f32(T)], [A.f32(T), A.f32(T)]
    bs2 = [Buf(), Buf()]
    bRe2, bIm2 = [[Buf(), Buf()], [Buf(), Buf()]], [[Buf(), Buf()], [Buf(), Buf()]]
    nunit = 0
    Cn = [A.f32(64) for _ in range(2)]
    bCn = Buf()
    bdrow = A.f32(128)
    bbd = Buf()
    BDB2 = [[[A.f32(128) for _ in range(2)] for _ in range(4)] for _ in range(2)]
    BDC2 = [[[A.f32(128) for _ in range(2)] for _ in range(4)] for _ in range(2)]
    bBD2 = [Buf(), Buf()]
    Rr2 = [[A.f32(T // MB) for _ in range(2)] for _ in range(2)]
    Ri2 = [[A.f32(T // MB) for _ in range(2)] for _ in range(2)]
    bRr2 = [[Buf(), Buf()], [Buf(), Buf()]]
    bRi2 = [[Buf(), Buf()], [Buf(), Buf()]]
    yt, t5, zt = A.f32(512), A.f32(512), A.f32(512)
    by = Buf()
    for gs in range(8):
        DMA(S, "sp", u, P[3072 + gs * 128:3072 + (gs + 1) * 128, :], [g.sb["P"]], [bu_])
        nacc = 0
        for d in range(2):
            Td = 2304 if d == 0 else T
            nblk = Td // MB
            for w, nm in enumerate(("s5_c_re", "s5_c_im")):
                DMA(S, "sp", Cn[w], I[nm][d, gs * 8:(gs + 1) * 8].rearrange("g h p -> (g h) p"), [], [bCn])
            for pb in range(4):
                qi = (gs * 2 + d) * 4 + pb
                us = nunit % 2
                nunit += 1
                sre, sim, bs = sre2[us], sim2[us], bs2[us]
                bRe, bIm = bRe2[us], bIm2[us]
                ball = bRe + bIm
                BDB, BDC, bBD = BDB2[us], BDC2[us], bBD2[us]
                Rr, Ri, bRr, bRi = Rr2[us], Ri2[us], bRr2[us], bRi2[us]
                for w in range(2):
                    srcB = (Bbr, Bbi)[w][:, qi * 16:(qi + 1) * 16]
                    TT(S, "dve", bdrow.rearrange("p (g h) -> p g h", h=16),
                       srcB.unsqueeze(1).broadcast_to([128, 8, 16]), bdm.unsqueeze(2).broadcast_to([128, 8, 16]),
                       ALU.mult, [bpar, g.b_const], [bbd])
                    TR(S, bank[7][:, 0:128], bdrow, ident, [bbd, g.b_const], [bk[7]])
                    CP(S, "act", BDB[pb][w], bank[7][:, 0:128], [bk[7]], [bBD])
                    srcC = Cn[w][:, pb * 16:(pb + 1) * 16]
                    TT(S, "dve", bdrow.rearrange("p (g h) -> p g h", h=16),
                       srcC.unsqueeze(1).broadcast_to([128, 8, 16]), bdm.unsqueeze(2).broadcast_to([128, 8, 16]),
                       ALU.mult, [bCn, g.b_const], [bbd])
                    TR(S, bank[7][:, 128:256], bdrow, ident, [bbd, g.b_const], [bk[7]])
                    if w == 0:
                        CP(S, "act", BDC[pb][w], bank[7][:, 128:256], [bk[7]], [bBD])
                    else:
                        ACT(S, BDC[pb][w], bank[7][:, 128:256], AF.Copy, [bk[7]], [bBD], scale=-1.0)
                for w, dst in enumerate((sre, sim)):
                    for tt0 in range(0, Td, 512):
                        n = min(512, Td - tt0)
                        pbk = 4 + (tt0 // 512 + w) % 2
                        MM(S, bank[pbk][:, 0:n], BDB[pb][w], u[:, tt0:tt0 + n], True, True, [bBD, bu_], [bk[pbk]])
                        CP(S, "act", dst[:, tt0:tt0 + n], bank[pbk][:, 0:n],
                           [bk[pbk]], [bs] + ball)
                r3 = sre[:, 0:Td].rearrange("p (j i) -> p j i", i=MB)
                i3 = sim[:, 0:Td].rearrange("p (j i) -> p j i", i=MB)
                c_lr, c_li, c_nli = lr[:, qi:qi + 1], li[:, qi:qi + 1], nli[:, qi:qi + 1]
                seq = range(1, MB) if d == 0 else range(MB - 2, -1, -1)
                for i in seq:
                    ip = i - 1 if d == 0 else i + 1
                    wr, wi, pr_, pi2 = bRe[i % 2], bIm[i % 2], bRe[ip % 2], bIm[ip % 2]
                    STT(S, "dve", r3[:, :, i], r3[:, :, ip], c_lr, r3[:, :, i], ALU.mult, ALU.add, [bs, bpar, pr_], [wr])
                    STT(S, "dve", i3[:, :, i], i3[:, :, ip], c_lr, i3[:, :, i], ALU.mult, ALU.add, [bs, bpar, pi2], [wi])
                    STT(S, "dve", r3[:, :, i], i3[:, :, ip], c_nli, r3[:, :, i], ALU.mult, ALU.add, [bs, bpar, pi2], [wr])
                    STT(S, "dve", i3[:, :, i], r3[:, :, ip], c_li, i3[:, :, i], ALU.mult, ALU.add, [bs, bpar, pr_], [wi])
                ie = MB - 1 if d == 0 else 0
                cur = 0
                CP(S, "dve", Rr[0][:, 0:nblk], r3[:, :, ie], [bs] + ball, [bRr[0]])
                CP(S, "dve", Ri[0][:, 0:nblk], i3[:, :, ie], [bs] + ball, [bRi[0]])
                segs = [(0, nblk)] if d == 0 else [(0, NCTX), (NCTX, nblk)]
                for si, (lo, hi) in enumerate(segs):
                    if d == 1 and si == 1:
                        a_r, a_i = Rr[cur], Ri[cur]
                        l0r, l0i, nl0i = (tb(z_, 0)[:, qi:qi + 1] for z_ in (LMr, LMi, nLMi))
                        e = hi - 1
                        bR = [bRr[cur], bRi[cur]]
                        STT(S, "dve", a_r[:, e:e + 1], a_r[:, 0:1], l0r, a_r[:, e:e + 1], ALU.mult, ALU.add, bR + [bpar], bR)
                        STT(S, "dve", a_r[:, e:e + 1], a_i[:, 0:1], nl0i, a_r[:, e:e + 1], ALU.mult, ALU.add, bR + [bpar], bR)
                        STT(S, "dve", a_i[:, e:e + 1], a_i[:, 0:1], l0r, a_i[:, e:e + 1], ALU.mult, ALU.add, bR + [bpar], bR)
                        STT(S, "dve", a_i[:, e:e + 1], a_r[:, 0:1], l0i, a_i[:, e:e + 1], ALU.mult, ALU.add, bR + [bpar], bR)
                    k = 0
                    sh = 1
                    while sh < hi - lo:
                        a_r, a_i, n_r, n_i = Rr[cur], Ri[cur], Rr[1 - cur], Ri[1 - cur]
                        pr, pi_, npi = (tb(z_, k)[:, qi:qi + 1] for z_ in (LMr, LMi, nLMi))
                        if d == 0:
                            dst_s, src_s, keep = slice(lo + sh, hi), slice(lo, hi - sh), slice(lo, lo + sh)
                        else:
                            dst_s, src_s, keep = slice(lo, hi - sh), slice(lo + sh, hi), slice(hi - sh, hi)
                        ar_b, ai_b, nr_b, ni_b = bRr[cur], bRi[cur], bRr[1 - cur], bRi[1 - cur]
                        CP(S, "act", n_r[:, 0:nblk], a_r[:, 0:nblk], [ar_b], [nr_b])
                        CP(S, "act", n_i[:, 0:nblk], a_i[:, 0:nblk], [ai_b], [ni_b])
                        STT(S, "dve", n_r[:, dst_s], a_r[:, src_s], pr, n_r[:, dst_s], ALU.mult, ALU.add, [ar_b, bpar], [nr_b])
                        STT(S, "dve", n_i[:, dst_s], a_i[:, src_s], pr, n_i[:, dst_s], ALU.mult, ALU.add, [ai_b, bpar], [ni_b])
                        STT(S, "dve", n_r[:, dst_s], a_i[:, src_s], npi, n_r[:, dst_s], ALU.mult, ALU.add, [ai_b, bpar], [nr_b])
                        STT(S, "dve", n_i[:, dst_s], a_r[:, src_s], pi_, n_i[:, dst_s], ALU.mult, ALU.add, [ar_b, bpar], [ni_b])
                        cur = 1 - cur
                        sh *= 2
                        k += 1
                a_r, a_i = Rr[cur], Ri[cur]
                if d == 0:
                    sin_r, sin_i = a_r[:, JB0 - 1:JB1 - 1], a_i[:, JB0 - 1:JB1 - 1]
                else:
                    sin_r, sin_i = a_r[:, JB0 + 1:JB1 + 1], a_i[:, JB0 + 1:JB1 + 1]
                for i in range(MB):
                    e = i if d == 0 else MB - 1 - i
                    pr, pi_, npi = (z_[:, e * NQ + qi:e * NQ + qi + 1] for z_ in (TBr, TBi, nTBi))
                    bRc = [bRr[cur], bRi[cur]]
                    if i < 2:
                        rdx = ball
                    else:
                        rdx = []
                    STT(S, "dve", r3[:, JB0:JB1, i], sin_r, pr, r3[:, JB0:JB1, i], ALU.mult, ALU.add, rdx + bRc + [bpar], [bRe[i % 2]])
                    STT(S, "dve", i3[:, JB0:JB1, i], sin_i, pr, i3[:, JB0:JB1, i], ALU.mult, ALU.add, rdx + bRc + [bpar], [bIm[i % 2]])
                    STT(S, "dve", r3[:, JB0:JB1, i], sin_i, npi, r3[:, JB0:JB1, i], ALU.mult, ALU.add, bRc + [bpar], [bRe[i % 2]])
                    STT(S, "dve", i3[:, JB0:JB1, i], sin_r, pi_, i3[:, JB0:JB1, i], ALU.mult, ALU.add, bRc + [bpar], [bIm[i % 2]])
                for ti in range(4):
                    osl = slice(OWN0 + ti * 512, OWN0 + (ti + 1) * 512)
                    MM(S, bank[ti], BDC[pb][0], sre[:, osl], nacc == 0, False, [bBD, bs] + ball, [bk[ti]])
                    MM(S, bank[ti], BDC[pb][1], sim[:, osl], False, nacc == 7, [bBD, bs] + ball, [bk[ti]])
                nacc += 1
        for ti in range(4):
            osl = slice(OWN0 + ti * 512, OWN0 + (ti + 1) * 512)
            STT(S, "dve", yt, u[:, osl], sdc[:, gs:gs + 1], bank[ti], ALU.mult, ALU.add, [bu_, bk[ti], g.b_pers], [by])
            gelu_tanh(S, zt, yt, t5, by)
            DMA(S, "pool", ZT[gs * 128:(gs + 1) * 128, ti * 512:(ti + 1) * 512], zt, [by], [g.sb["ZT"]])


def gelu_tanh(S, out, y, tmp, b):
    ACT(S, tmp, y, AF.Square, [b], [b])
    TS(S, "dve", tmp, tmp, 0.044715, 1.0, ALU.mult, ALU.add, [b], [b])
    TT(S, "dve", tmp, tmp, y, ALU.mult, [b], [b])
    ACT(S, tmp, tmp, AF.Sigmoid, [b], [b], scale=1.5957691216057308)
    TT(S, "dve", out, y, tmp, ALU.mult, [b], [b])


def rstd_from(S, x_, st, bx, bst, junk, bjunk, eps=1e-6):
    ACT(S, junk, x_, AF.Square, [bx], [bjunk, bst], accum_out=st[:, 0:1])
    TS(S, "dve", st[:, 1:2], st[:, 0:1], 1.0 / D, eps, ALU.mult, ALU.add, [bst], [bst])
    ACT(S, st[:, 2:3], st[:, 1:2], AF.Sqrt, [bst], [bst])
    S.op("dve", lambda e, o=st[:, 3:4], i_=st[:, 2:3]: e.reciprocal(o, i_), [bst], [bst])


def stage_tail(g):
    S, nc, A = g.S, g.nc, g.arena
    I = g.inp
    OT, ZT = g.scr["OT"], g.scr["ZT"]
    bk = [Buf("bank%d" % i) for i in range(8)]
    bank = [g.ps[:, 512 * i:512 * (i + 1)] for i in range(8)]
    A.reset()
    gw = A.f32(8 * 1024)
    gw3 = gw.rearrange("p (k n) -> p k n", n=1024)
    bgw = Buf()
    DMA(S, "sp", gw3, I["s5_glu_w"].rearrange("(k p) n -> p k n", p=128), [], [bgw])
    gbc = g.pers.f32(8)
    colvec(g, I["s5_glu_b"].rearrange("(k p) -> k p", p=128), 8, gbc)
    zt = [A.f32(8 * 512) for _ in range(2)]
    bz = [Buf(), Buf()]
    gt_ = [A.f32(512) for _ in range(2)]
    og = [A.bf16(512) for _ in range(2)]
    bg = [Buf(), Buf()]
    ZTv = ZT.rearrange("(k p) t -> p k t", p=128)
    n_ = 0
    for ti in range(4):
        z3 = zt[ti % 2].rearrange("p (k t) -> p k t", t=512)
        DMA(S, "sp", z3, ZTv[:, :, ti * 512:(ti + 1) * 512], [g.sb["ZT"]], [bz[ti % 2]])
        for nc_ in range(8):
            pb = n_ % 4
            for kc in range(8):
                MM(S, bank[pb], gw3[:, kc, nc_ * 128:(nc_ + 1) * 128], z3[:, kc, :],
                   kc == 0, kc == 7, [bgw, bz[ti % 2]], [bk[pb]])
            ACT(S, gt_[n_ % 2], bank[pb], AF.Sigmoid, [bk[pb], g.b_pers], [bg[n_ % 2]], bias=gbc[:, nc_:nc_ + 1])
            TT(S, "dve", og[n_ % 2], z3[:, nc_, :], gt_[n_ % 2], ALU.mult, [bz[ti % 2], bg[n_ % 2]], [bg[n_ % 2]])
            DMA(S, "pool", OT[1024 + nc_ * 128:1024 + (nc_ + 1) * 128, ti * 512:(ti + 1) * 512], og[n_ % 2],
                [bg[n_ % 2]], [g.sb["OT"]])
            n_ += 1
    A.reset()
    X1 = scratch(g, "X1", [NOWN, D], F32)
    H2T = scratch(g, "H2T", [16, 128, NOWN], BF16)
    Wo = A.bf16(16 * 2048)
    Wo3 = Wo.rearrange("p (k n) -> p k n", n=2048)
    bWo = Buf()
    wst = [A.f32(2048) for _ in range(2)]
    bwst = [Buf(), Buf()]
    for mc in range(16):
        DMA(S, "sp" if mc % 2 == 0 else "pool", wst[mc % 2], I["w_out"][mc * 128:(mc + 1) * 128, :], [], [bwst[mc % 2]])
        CP(S, "act" if mc % 2 == 0 else "pool", Wo3[:, mc, :], wst[mc % 2], [bwst[mc % 2]], [bWo])
    gtr = A.f32(2048)
    brow = Buf()
    DMA(S, "sp", gtr, g.scr["gates"][0].rearrange("a b -> (a b)").partition_broadcast(128), [g.sb["gates"]], [brow])
    xt = [A.f32(2048) for _ in range(2)]
    bxt = [Buf(), Buf()]
    x1 = [A.f32(2048) for _ in range(2)]
    bx1 = [Buf(), Buf()]
    xn = [A.bf16(2048) for _ in range(2)]
    bxn = [Buf(), Buf()]
    junk = A.bf16(2048)
    bjunk = Buf()
    st4 = [A.f32(8) for _ in range(2)]
    bst = [Buf(), Buf()]
    ot = [A.bf16(16 * 128) for _ in range(2)]
    bot = [Buf(), Buf()]
    h2 = [A.bf16(16 * 128) for _ in range(2)]
    bh2 = [Buf(), Buf()]
    OTv = OT.rearrange("(k p) t -> p k t", p=128)
    H2v = H2T.rearrange("k p t -> p k t")
    ptr = g.ps[:, 2048:3072].bitcast(BF16)
    bptr = Buf()
    for su in range(16):
        i2 = su % 2
        o3 = ot[i2].rearrange("p (k t) -> p k t", t=128)
        DMA(S, "sp", o3, OTv[:, :, su * 128:(su + 1) * 128], [g.sb["OT"]], [bot[i2]])
        DMA(S, "pool", xt[i2], I["x"][su * 128:(su + 1) * 128, :], [], [bxt[i2]])
        for db in range(4):
            for mc in range(16):
                MM(S, bank[db], o3[:, mc, :], Wo3[:, mc, db * 512:(db + 1) * 512], mc == 0, mc == 15,
                   [bot[i2], bWo], [bk[db]])
            dsl = slice(db * 512, (db + 1) * 512)
            TT(S, "dve", x1[i2][:, dsl], bank[db], gtr[:, dsl], ALU.mult, [bk[db], brow], [bx1[i2]])
            TT(S, "pool", x1[i2][:, dsl], x1[i2][:, dsl], xt[i2][:, dsl], ALU.add, [bx1[i2], bxt[i2]], [bx1[i2]])
        DMA(S, "pool", X1[su * 128:(su + 1) * 128, :], x1[i2], [bx1[i2]], [g.sb["X1"]])
        rstd_from(S, x1[i2], st4[i2], bx1[i2], bst[i2], junk, bjunk)
        TS(S, "dve", xn[i2], x1[i2], st4[i2][:, 3:4], None, ALU.mult, None, [bx1[i2], bst[i2]], [bxn[i2]])
        for kc in range(16):
            TR(S, ptr[:, kc * 128:(kc + 1) * 128], xn[i2][:, kc * 128:(kc + 1) * 128], g.identb,
               [bxn[i2], g.b_const], [bptr])
        h3 = h2[i2].rearrange("p (k t) -> p k t", t=128)
        for kc in range(16):
            srcp = ptr[:, kc * 128:(kc + 1) * 128]
            if kc % 2 == 0:
                TS(S, "dve", h3[:, kc, :], srcp, g.A2x[:, kc:kc + 1], g.B2x[:, kc:kc + 1], ALU.mult, ALU.add,
                   [bptr, g.b_pers], [bh2[i2]])
            else:
                ACT(S, h3[:, kc, :], srcp, AF.Identity, [bptr, g.b_pers], [bh2[i2]],
                    bias=g.B2x[:, kc:kc + 1], scale=g.A2x[:, kc:kc + 1])
        DMA(S, "sp", H2v[:, :, su * 128:(su + 1) * 128], h3, [bh2[i2]], [g.sb["H2T"]])
    A.reset()
    cw = [g.pers.f32(44) for _ in range(3)]
    cb = g.pers.f32(44)
    for i in range(3):
        colvec(g, I["ffn_conv_w"][i].rearrange("(k p) -> k p", p=128), 44, cw[i])
    colvec(g, I["ffn_conv_b"].rearrange("(k p) -> k p", p=128), 44, cb)
    gfr, fgr = A.f32(2048), A.f32(2048)
    brow2 = Buf()
    DMA(S, "sp", gfr, g.scr["gates"][1].rearrange("a b -> (a b)").partition_broadcast(128), [g.sb["gates"]], [brow2])
    DMA(S, "sp", fgr, I["final_norm_g"].partition_broadcast(128), [], [brow2])
    h2t = A.bf16(16 * 512)
    h23 = h2t.rearrange("p (k t) -> p k t", t=512)
    bh = Buf()
    gT = A.bf16(44 * 512)
    gT3 = gT.rearrange("p (f t) -> p f t", t=512)
    bgT = Buf()
    wuf = [A.f32(16 * 128) for _ in range(3)]
    bwuf = [Buf() for _ in range(3)]
    wub = [A.bf16(16 * 128) for _ in range(3)]
    bwub = [Buf() for _ in range(3)]
    wdf = [A.f32(512) for _ in range(3)]
    bwdf = [Buf() for _ in range(3)]
    wdb = [A.bf16(512) for _ in range(3)]
    bwdb = [Buf() for _ in range(3)]
    WUB = scratch(g, "WUB", [88, 128, 16 * 128], BF16)
    bWUB = [Buf() for _ in range(88)]
    WDB = scratch(g, "WDB", [176, 128, 512], BF16)
    bWDB = [Buf() for _ in range(176)]
    xs = [A.f32(2048) for _ in range(4)]
    bxs = [Buf() for _ in range(4)]
    cg, t5, gl = A.f32(512), A.f32(512), A.f32(512)
    bcg = Buf()
    junk = A.bf16(2048)
    bjunk = Buf()
    st = A.f32(8)
    bst_ = Buf()
    wuv = I["ffn_w_up"].rearrange("(kc p) n -> p kc n", p=128)
    nw = 0
    nd = 0
    for ti in range(4):
        DMA(S, "sp", h23, g.scr["H2T"].rearrange("k p t -> p k t")[:, :, ti * 512:(ti + 1) * 512], [g.sb["H2T"]], [bh])
        for su in range(4):
            r_ = ti * 512 + su * 128
            DMA(S, "pool", xs[su], X1[r_:r_ + 128, :], [g.sb["X1"]], [bxs[su]])
        for fc in range(44):
            pbs = []
            for half in range(2):
                c0 = half * DFF + fc * 128
                wb_ = wub[nw % 3]
                bwb_ = bwub[nw % 3]
                wb3 = wb_.rearrange("p (k m) -> p k m", m=128)
                wi = fc * 2 + half
                if ti == 0:
                    wf_ = wuf[nw % 3]
                    wf3 = wf_.rearrange("p (k m) -> p k m", m=128)
                    DMA(S, "sp" if nw % 2 == 0 else "pool", wf3, wuv[:, :, c0:c0 + 128], [], [bwuf[nw % 3]])
                    CP(S, "pool" if nw % 2 == 0 else "act", wb_, wf_, [bwuf[nw % 3]], [bwb_])
                    DMA(S, "pool" if nw % 2 == 0 else "sp", WUB[wi], wb_, [bwb_], [bWUB[wi]])
                else:
                    DMA(S, "sp" if nw % 2 == 0 else "pool", wb_, WUB[wi], [bWUB[wi]], [bwb_])
                pb = nw % 4
                for kc in range(16):
                    MM(S, bank[pb], wb3[:, kc, :], h23[:, kc, :], kc == 0, kc == 15, [bwb_, bh], [bk[pb]])
                pbs.append(pb)
                nw += 1
            pg, pv = pbs
            ACT(S, cg, bank[pg], AF.Identity, [bk[pg], g.b_pers], [bcg], bias=cb[:, fc:fc + 1], scale=cw[1][:, fc:fc + 1])
            c3 = cg.rearrange("p (r t) -> p r t", t=64)
            p3 = bank[pg].rearrange("p (r t) -> p r t", t=64)
            STT(S, "dve", c3[:, :, 1:64], p3[:, :, 0:63], cw[0][:, fc:fc + 1], c3[:, :, 1:64], ALU.mult, ALU.add,
                [bk[pg], g.b_pers, bcg], [bcg])
            STT(S, "dve", c3[:, :, 0:63], p3[:, :, 1:64], cw[2][:, fc:fc + 1], c3[:, :, 0:63], ALU.mult, ALU.add,
                [bk[pg], g.b_pers, bcg], [bcg])
            gelu_tanh(S, gl, cg, t5, bcg)
            TT(S, "dve", gT3[:, fc, :], gl, bank[pv], ALU.mult, [bcg, bk[pv]], [bgT])
        for db in range(4):
            dsl = slice(db * 512, (db + 1) * 512)
            for fc in range(44):
                wb_ = wdb[nd % 3]
                bwb_ = bwdb[nd % 3]
                wi = db * 44 + fc
                if ti == 0:
                    wf_ = wdf[nd % 3]
                    DMA(S, "sp" if nd % 2 == 0 else "pool", wf_, I["ffn_w_down"][fc * 128:(fc + 1) * 128, dsl], [], [bwdf[nd % 3]])
                    CP(S, "act" if nd % 2 == 0 else "pool", wb_, wf_, [bwdf[nd % 3]], [bwb_])
                    DMA(S, "pool" if nd % 2 == 0 else "sp", WDB[wi], wb_, [bwb_], [bWDB[wi]])
                else:
                    DMA(S, "sp" if nd % 2 == 0 else "pool", wb_, WDB[wi], [bWDB[wi]], [bwb_])
                for su in range(4):
                    MM(S, bank[4 + su], gT3[:, fc, su * 128:(su + 1) * 128], wb_, fc == 0, fc == 43,
                       [bgT, bwb_], [bk[4 + su]])
                nd += 1
            for su in range(4):
                TT(S, "dve", t5, bank[4 + su], gfr[:, dsl], ALU.mult, [bk[4 + su], brow2, bcg], [bcg])
                TT(S, "pool", xs[su][:, dsl], xs[su][:, dsl], t5, ALU.add, [bxs[su], bcg], [bxs[su]])
        for su in range(4):
            rstd_from(S, xs[su], st, bxs[su], bst_, junk, bjunk)
            TS(S, "dve", xs[su], xs[su], st[:, 3:4], None, ALU.mult, None, [bxs[su], bst_], [bxs[su]])
            TT(S, "pool", xs[su], xs[su], fgr, ALU.mult, [bxs[su], brow2], [bxs[su]])
            r_ = ti * 512 + su * 128
            DMA(S, "sp", g.out[r_:r_ + 128, :], xs[su], [bxs[su]], [Buf()])
```

```python
import numpy as np
import concourse.bass as bass
import concourse.mybir as mybir
from concourse.bass_utils import run_bass_kernel_spmd

F32 = mybir.dt.float32
BF16 = mybir.dt.bfloat16
ALU = mybir.AluOpType
AF = mybir.ActivationFunctionType

DEBUG = False
STOP_AFTER = 99
SKIP_RWKV = False
S5_MB = 8
RWKV_BF16 = True
SAME_ENGINE_INORDER = ("pe",)

T = 4352
D = 2048
NIN = 4512
TILES = [(0, 256)] + [(256 + 512 * i, 512) for i in range(8)]
OWN0, OWN1 = 256, 2304
NOWN = 2048
DFF = 5632
CHUNKS = [(128 * i, 128, "rkv") for i in range(24)]
CHUNKS += [(3072, 128, "gd0"), (3200, 32, "gd1"), (3232, 64, "wd0"), (3296, 64, "wd1"),
           (3360, 64, "ad0"), (3424, 64, "ad1")]
CHUNKS += [(3488 + 128 * i, 128, "u") for i in range(8)]


class Buf:
    __slots__ = ("name", "w", "r")

    def __init__(self, name=""):
        self.name = name
        self.w = None
        self.r = []


class Sched:
    EPOCH = 30000
    NDMA = 6

    def __init__(self, nc):
        self.nc = nc
        self.engines = ["pe", "dve", "act", "pool", "sp"]
        self.stream = {e: [] for e in self.engines}
        self.cnt = {e: 0 for e in self.engines}
        self.sems = {}
        self.seen = {e: {} for e in self.engines}
        self.dma_n = {e: 0 for e in self.engines}
        self.dma_tok = {e: [None] * self.NDMA for e in self.engines}
        self.last = {}
        self.pending = {e: [] for e in self.engines}

    def barrier(self):
        toks = list(self.last.values())
        for e in self.engines:
            self.pending[e] = list(toks)

    def _sem(self, key):
        if key not in self.sems:
            self.sems[key] = self.nc.alloc_semaphore(name="s_%s_%s" % key)
        return self.sems[key]

    def _waits(self, eng, deps):
        need = {}
        for t in deps:
            if t is None:
                continue
            key, val, teng, is_dma = t
            if (not is_dma) and teng == eng and eng in SAME_ENGINE_INORDER:
                continue
            if self.seen[eng].get(key, 0) >= val:
                continue
            if need.get(key, 0) < val:
                need[key] = val
        for key, val in need.items():
            self.seen[eng][key] = val
        return [(self._sem(key), val) for key, val in need.items()]

    @staticmethod
    def _deps(reads, writes):
        deps = []
        for b in reads:
            deps.append(b.w)
        for b in writes:
            deps.append(b.w)
            deps.extend(b.r)
        return deps

    def _commit(self, tok, reads, writes):
        for b in reads:
            b.r.append(tok)
            if len(b.r) > 48:
                b.r = b.r[-48:]
        for b in writes:
            b.w = tok
            b.r = []
        self.last[tok[0]] = tok

    def op(self, eng, fn, reads=(), writes=()):
        deps = self._deps(reads, writes)
        if self.pending[eng]:
            deps.extend(self.pending[eng])
            self.pending[eng] = []
        waits = self._waits(eng, deps)
        n = self.cnt[eng]
        key = (eng, n // self.EPOCH)
        val = n % self.EPOCH + 1
        self.cnt[eng] = n + 1
        self.stream[eng].append((waits, fn, self._sem(key), 1))
        tok = (key, val, eng, False)
        self._commit(tok, reads, writes)
        return tok

    def dma(self, eng, fn, reads=(), writes=()):
        i = self.dma_n[eng]
        self.dma_n[eng] = i + 1
        slot = i % self.NDMA
        deps = self._deps(reads, writes)
        if self.pending[eng]:
            deps.extend(self.pending[eng])
            self.pending[eng] = []
        deps.append(self.dma_tok[eng][slot])
        waits = self._waits(eng, deps)
        key = ("d" + eng, slot)
        val = 16 * (i // self.NDMA + 1)
        self.stream[eng].append((waits, fn, self._sem(key), 16))
        tok = (key, val, eng, True)
        self.dma_tok[eng][slot] = tok
        self._commit(tok, reads, writes)
        return tok

    def finish(self, final_eng="sp"):
        waits = self._waits(final_eng, list(self.last.values()))
        self.stream[final_eng].append((waits, None, None, 0))
        streams = self.stream

        def replay(e, name):
            for waits, fn, sem, inc in streams[name]:
                for s, v in waits:
                    e.wait_ge(s, v)
                if fn is not None:
                    fn(e).then_inc(sem, inc)

        with self.nc.Block() as block:
            @block.tensor
            def _(e):
                replay(e, "pe")

            @block.vector
            def _(e):
                replay(e, "dve")

            @block.scalar
            def _(e):
                replay(e, "act")

            @block.gpsimd
            def _(e):
                replay(e, "pool")

            @block.sync
            def _(e):
                replay(e, "sp")


class Arena:
    def __init__(self, ap, width, sched=None):
        self.ap = ap
        self.width = width
        self.off = 0
        self.sched = sched

    def reset(self):
        self.off = 0
        if self.sched is not None:
            self.sched.barrier()

    def f32(self, n):
        nr = (n + 7) // 8 * 8
        assert self.off + nr <= self.width, ("arena overflow", self.off, nr, self.width)
        a = self.ap[:, self.off:self.off + n]
        self.off += nr
        return a

    def bf16(self, n):
        return self.f32((n + 1) // 2).bitcast(BF16)[:, 0:n]


class G:
    pass


def MM(S, out, lhsT, rhs, start, stop, reads, writes):
    return S.op("pe", lambda e: e.matmul(out, lhsT=lhsT, rhs=rhs, start=start, stop=stop), reads, writes)


def TR(S, out, in_, ident, reads, writes):
    return S.op("pe", lambda e: e.transpose(out, in_, ident), reads, writes)


def ACT(S, out, in_, func, reads, writes, bias=0.0, scale=1.0, accum_out=None, eng="act"):
    if accum_out is None:
        return S.op(eng, lambda e: e.activation(out=out, in_=in_, func=func, bias=bias, scale=scale), reads, writes)
    return S.op(eng, lambda e: e.activation(out=out, in_=in_, func=func, bias=bias, scale=scale,
                                            accum_out=accum_out), reads, writes)


def TS(S, eng, out, in0, s1, s2, op0, op1, reads, writes):
    if s2 is None:
        return S.op(eng, lambda e: e.tensor_scalar(out, in0, s1, None, op0), reads, writes)
    return S.op(eng, lambda e: e.tensor_scalar(out, in0, s1, s2, op0, op1), reads, writes)


def TT(S, eng, out, in0, in1, op, reads, writes):
    return S.op(eng, lambda e: e.tensor_tensor(out, in0, in1, op), reads, writes)


def STT(S, eng, out, in0, scalar, in1, op0, op1, reads, writes):
    return S.op(eng, lambda e: e.scalar_tensor_tensor(out, in0, scalar, in1, op0, op1), reads, writes)


def CP(S, eng, out, in_, reads, writes):
    if eng == "act":
        return S.op(eng, lambda e: e.copy(out, in_), reads, writes)
    return S.op(eng, lambda e: e.tensor_copy(out, in_), reads, writes)


def DMA(S, q, out, in_, reads, writes, slow=False):
    q = "pool" if str(out.space).endswith("DRAM") else "sp"
    if slow:
        return S.dma(q, lambda e: e.dma_start(out=out, in_=in_, allow_slow_non_contiguous=True), reads, writes)
    return S.dma(q, lambda e: e.dma_start(out=out, in_=in_), reads, writes)


def scratch(g, name, shape, dt):
    kind = "ExternalOutput" if (DEBUG and name in g.dbg) else "Internal"
    t = g.nc.dram_tensor(name, list(shape), dt, kind=kind).ap()
    g.scr[name] = t
    g.sb[name] = Buf(name)
    return t


def colvec(g, dram2d, rows, dst, q="sp"):
    S = g.S
    st = g.cv_stage
    DMA(S, q, st[0:rows, :], dram2d, [], [g.b_cvs])
    TR(S, g.ps[:, 3584:3584 + rows], st[0:rows, :], g.ident[0:rows, 0:rows], [g.b_cvs, g.b_const], [g.b_cvp])
    CP(S, "dve", dst, g.ps[:, 3584:3584 + rows], [g.b_cvp], [g.b_pers])


def stage_mod(g):
    S, nc, A = g.S, g.nc, g.arena
    A.reset()
    I = g.inp
    sc = g.pers.f32(32)
    sc3 = sc.rearrange("p (k t) -> p k t", t=2)
    tmp = g.pers.f32(32)
    colvec(g, I["c"].rearrange("(k p) -> k p", p=128), 16, tmp[:, 0:16])
    colvec(g, I["c_ctx"].rearrange("(k p) -> k p", p=128), 16, tmp[:, 16:32])
    ACT(S, sc3[:, :, 0], tmp[:, 0:16], AF.Silu, [g.b_pers], [g.b_pers])
    ACT(S, sc3[:, :, 1], tmp[:, 16:32], AF.Silu, [g.b_pers], [g.b_pers])
    modb = g.pers.f32(96)
    colvec(g, I["mod_b"].rearrange("(k p) -> k p", p=128), 96, modb)
    wv = I["mod_w"].rearrange("(kc p) n -> p kc n", p=128)
    NB = 3
    wt = [A.f32(16 * 512) for _ in range(NB)]
    bw = [Buf() for _ in range(NB)]
    bps = Buf()
    pm = g.ps[:, 0:192]
    rowm = A.f32(6 * D)
    brow = Buf()
    bkm = [Buf() for _ in range(4)]
    for nb in range(24):
        w = wt[nb % NB]
        w3 = w.rearrange("p (k n) -> p k n", n=512)
        DMA(S, "sp", w3, wv[:, :, nb * 512:(nb + 1) * 512], [], [bw[nb % NB]])
        pbk = g.ps[0:2, 512 * (1 + nb % 4):512 * (2 + nb % 4)]
        for kc in range(16):
            MM(S, pbk, sc3[:, kc, :], w3[:, kc, :], kc == 0, kc == 15, [bw[nb % NB], g.b_pers], [bkm[nb % 4]])
        CP(S, "act" if nb % 2 == 0 else "dve", rowm[0:2, nb * 512:(nb + 1) * 512], pbk, [bkm[nb % 4]], [brow])
    for ch in range(96):
        TR(S, pm[:, ch * 2:ch * 2 + 2], rowm[0:2, ch * 128:(ch + 1) * 128], g.ident[0:2, 0:2],
           [brow, g.b_const], [bps])
    pm3 = pm.rearrange("p (c t) -> p c t", t=2)
    g.modx = g.pers.f32(96)
    g.modc = g.pers.f32(96)
    TT(S, "dve", g.modx, pm3[:, :, 0], modb, ALU.add, [bps, g.b_pers], [g.b_pers])
    TT(S, "dve", g.modc, pm3[:, :, 1], modb, ALU.add, [bps, g.b_pers], [g.b_pers])
    gm = g.pers.f32(16)
    gf = g.pers.f32(16)
    colvec(g, I["norm_mix_g"].rearrange("(k p) -> k p", p=128), 16, gm)
    colvec(g, I["norm_ffn_g"].rearrange("(k p) -> k p", p=128), 16, gf)
    g.A1x, g.A1c, g.A2x = g.pers.f32(16), g.pers.f32(16), g.pers.f32(16)
    for dst, mod, off, gain in ((g.A1x, g.modx, 16, gm), (g.A1c, g.modc, 16, gm), (g.A2x, g.modx, 64, gf)):
        STT(S, "dve", dst, mod[:, off:off + 16], 1.0, gain, ALU.add, ALU.mult, [g.b_pers], [g.b_pers])
    g.B1x, g.B1c, g.B2x = g.modx[:, 0:16], g.modc[:, 0:16], g.modx[:, 48:64]
    gsc = scratch(g, "gates", [2, 16, 128], F32)
    rows = g.pers.f32(256)
    for i, off in enumerate((32, 80)):
        TR(S, g.ps[0:16, 3584:3712], g.modx[:, off:off + 16], g.ident, [g.b_pers, g.b_const], [g.b_cvp])
        CP(S, "dve", rows[0:16, i * 128:(i + 1) * 128], g.ps[0:16, 3584:3712], [g.b_cvp], [g.b_pers])
        DMA(S, "sp", gsc[i], rows[0:16, i * 128:(i + 1) * 128], [g.b_pers], [g.sb["gates"]])


def stage_proj(g):
    S, nc, A = g.S, g.nc, g.arena
    A.reset()
    I = g.inp
    P = scratch(g, "P", [4096 + 0, T], F32)
    LW = scratch(g, "LOGW", [2, 1024, T], F32)
    AAs = scratch(g, "AA", [2, 1024, T], F32)
    GG = scratch(g, "GG", [1024, NOWN], F32)
    wup = [A.f32(1024) for _ in range(2)]
    aup = [A.f32(1024) for _ in range(2)]
    gup0, gup1 = A.f32(1024), A.f32(1024)
    bsm = Buf()
    for d in range(2):
        DMA(S, "sp", wup[d][0:64, :], I["rwkv_w_up"][d], [], [bsm])
        DMA(S, "sp", aup[d][0:64, :], I["rwkv_a_up"][d], [], [bsm])
    DMA(S, "sp", gup0, I["rwkv_g_up"][0:128, :], [], [bsm])
    DMA(S, "sp", gup1[0:32, :], I["rwkv_g_up"][128:160, :], [], [bsm])
    w0c = [g.pers.f32(8) for _ in range(2)]
    a0c = [g.pers.f32(8) for _ in range(2)]
    for d in range(2):
        colvec(g, I["rwkv_w0"][d].rearrange("(k p) -> k p", p=128), 8, w0c[d])
        colvec(g, I["rwkv_a0"][d].rearrange("(k p) -> k p", p=128), 8, a0c[d])
    NCH = len(CHUNKS)
    mu0, mu1, muc = g.pers.f32(NCH), g.pers.f32(NCH), g.pers.f32(NCH)
    S.op("dve", lambda e: e.memset(mu0, 0.0), [], [g.b_pers])
    S.op("dve", lambda e: e.memset(mu1, 0.0), [], [g.b_pers])
    colvec(g, I["shift_mu"][0, 0:3072].rearrange("(k p) -> k p", p=128), 24, mu0[:, 0:24])
    colvec(g, I["shift_mu"][1, 0:3072].rearrange("(k p) -> k p", p=128), 24, mu1[:, 0:24])
    for ci in range(24, 30):
        c0, M, _ = CHUNKS[ci]
        for mu, r in ((mu0, 0), (mu1, 1)):
            DMA(S, "sp", mu[0:M, ci:ci + 1], I["shift_mu"][r, c0:c0 + M].rearrange("(p o) -> p o", o=1),
                [], [g.b_pers], slow=True)
    TT(S, "dve", muc, mu0, mu1, ALU.add, [g.b_pers], [g.b_pers])
    TS(S, "dve", muc, muc, -1.0, 1.0, ALU.mult, ALU.add, [g.b_pers], [g.b_pers])

    xt = [A.f32(2048) for _ in range(2)]
    bxt = [Buf() for _ in range(2)]
    junk = A.bf16(2048)
    bjunk = Buf()
    xn = [A.bf16(2048) for _ in range(2)]
    bxn = [Buf() for _ in range(2)]
    st = A.f32(8)
    bst = Buf()
    hT = [A.bf16(16 * 512) for _ in range(2)]
    bhT = [Buf() for _ in range(2)]
    NWB = 3
    wf = [A.f32(16 * 128) for _ in range(NWB)]
    bwf = [Buf() for _ in range(NWB)]
    wb = [A.bf16(16 * 128) for _ in range(3)]
    bwb = [Buf() for _ in range(3)]
    WB = scratch(g, "WB", [len(CHUNKS), 128, 16 * 128], BF16)
    bWB = [Buf() for _ in range(len(CHUNKS))]
    ob = [A.f32(512) for _ in range(3)]
    bob = [Buf() for _ in range(3)]
    lora = {k: A.f32(512) for k in ("gd0", "gd1", "wd0", "wd1", "ad0", "ad1")}
    blora = Buf()
    ptr = g.ps[:, 0:1024].bitcast(BF16)
    bptr = Buf()
    pp = [g.ps[:, 1024 + 512 * i:1536 + 512 * i] for i in range(3)]
    bpp = [Buf() for _ in range(3)]
    pl = [g.ps[:, 2560 + 512 * i:3072 + 512 * i] for i in range(2)]
    bpl = [Buf() for _ in range(2)]
    wv = I["w_in"].rearrange("(kc p) n -> p kc n", p=128)
    nchunk = 0
    nl = 0
    cnt = {"nsub": 0}

    def prep_sub(ti, sub):
        t0, n = TILES[ti]
        nsub = cnt["nsub"]
        h3 = hT[ti % 2].rearrange("p (k t) -> p k t", t=512)
        A1, B1 = (g.A1c, g.B1c) if ti == 0 else (g.A1x, g.B1x)
        x_ = xt[nsub % 2]
        bx = bxt[nsub % 2]
        src = I["ctx"][sub * 128:(sub + 1) * 128, :] if ti == 0 else \
            I["x"][t0 - 256 + sub * 128:t0 - 256 + (sub + 1) * 128, :]
        DMA(S, "sp", x_, src, [], [bx])
        ACT(S, junk, x_, AF.Square, [bx], [bjunk, bst], accum_out=st[:, 0:1])
        TS(S, "dve", st[:, 1:2], st[:, 0:1], 1.0 / D, 1e-6, ALU.mult, ALU.add, [bst], [bst])
        ACT(S, st[:, 2:3], st[:, 1:2], AF.Sqrt, [bst], [bst])
        S.op("dve", lambda e, o=st[:, 3:4], i_=st[:, 2:3]: e.reciprocal(o, i_), [bst], [bst])
        xb = xn[nsub % 2]
        TS(S, "dve", xb, x_, st[:, 3:4], None, ALU.mult, None, [bx, bst], [bxn[nsub % 2]])
        for kc in range(16):
            TR(S, ptr[:, kc * 128:(kc + 1) * 128], xb[:, kc * 128:(kc + 1) * 128], g.identb,
               [bxn[nsub % 2], g.b_const], [bptr])
        for kc in range(16):
            dst = h3[:, kc, sub * 128:(sub + 1) * 128]
            srcp = ptr[:, kc * 128:(kc + 1) * 128]
            if kc % 2 == 0:
                TS(S, "dve", dst, srcp, A1[:, kc:kc + 1], B1[:, kc:kc + 1], ALU.mult, ALU.add,
                   [bptr, g.b_pers], [bhT[ti % 2]])
            else:
                ACT(S, dst, srcp, AF.Identity, [bptr, g.b_pers], [bhT[ti % 2]],
                    bias=B1[:, kc:kc + 1], scale=A1[:, kc:kc + 1])
        cnt["nsub"] = nsub + 1

    for sub in range(TILES[0][1] // 128):
        prep_sub(0, sub)
    for ti, (t0, n) in enumerate(TILES):
        h3 = hT[ti % 2].rearrange("p (k t) -> p k t", t=512)
        nxt = [(ti + 1, s_) for s_ in range(TILES[ti + 1][1] // 128)] if ti + 1 < len(TILES) else []
        rows = n // 64 if ti > 0 else 1
        rl = n // rows
        for ci, (c0, M, kind) in enumerate(CHUNKS):
            wb_ = wb[nchunk % 3]
            bwb_ = bwb[nchunk % 3]
            wb3 = wb_.rearrange("p (k m) -> p k m", m=128)
            qd = "sp" if nchunk % 2 == 0 else "pool"
            if ti == 0:
                w_ = wf[nchunk % NWB]
                w3 = w_.rearrange("p (k m) -> p k m", m=128)
                DMA(S, qd, w3[:, :, 0:M], wv[:, :, c0:c0 + M], [], [bwf[nchunk % NWB]])
                CP(S, "pool" if nchunk % 2 == 0 else "act", wb3[:, :, 0:M], w3[:, :, 0:M],
                   [bwf[nchunk % NWB]], [bwb_])
                DMA(S, "pool" if nchunk % 2 == 0 else "sp", WB[ci], wb_, [bwb_], [bWB[ci]])
            else:
                DMA(S, qd, wb_, WB[ci], [bWB[ci]], [bwb_])
            p_ = pp[nchunk % 3]
            bp = bpp[nchunk % 3]
            for kc in range(16):
                MM(S, p_[0:M, 0:n], wb3[:, kc, 0:M], h3[:, kc, 0:n], kc == 0, kc == 15,
                   [bwb_, bhT[ti % 2]], [bp])
            if kind in ("rkv", "u"):
                o_ = ob[nchunk % 3]
                bo = bob[nchunk % 3]
            else:
                o_ = lora[kind]
                bo = blora
            if kind == "u":
                CP(S, "act", o_[0:M, 0:n], p_[0:M, 0:n], [bp], [bo])
            else:
                ACT(S, o_[0:M, 0:n], p_[0:M, 0:n], AF.Copy, [bp, g.b_pers], [bo], scale=muc[0:M, ci:ci + 1])
                o3 = o_[0:M, 0:n].rearrange("p (r t) -> p r t", t=rl)
                p3 = p_[0:M, 0:n].rearrange("p (r t) -> p r t", t=rl)
                STT(S, "dve", o3[:, :, 1:rl], p3[:, :, 0:rl - 1], mu0[0:M, ci:ci + 1], o3[:, :, 1:rl],
                    ALU.mult, ALU.add, [bp, g.b_pers, bo], [bo])
                STT(S, "dve", o3[:, :, 0:rl - 1], p3[:, :, 1:rl], mu1[0:M, ci:ci + 1], o3[:, :, 0:rl - 1],
                    ALU.mult, ALU.add, [bp, g.b_pers, bo], [bo])
            if kind == "rkv":
                DMA(S, "pool", P[c0:c0 + M, t0:t0 + n], o_[0:M, 0:n], [bo], [g.sb["P"]])
            elif kind == "u":
                r0 = 3072 + (c0 - 3488)
                DMA(S, "pool", P[r0:r0 + M, t0:t0 + n], o_[0:M, 0:n], [bo], [g.sb["P"]])
            elif kind in ("gd0", "gd1"):
                ACT(S, o_[0:M, 0:n], o_[0:M, 0:n], AF.Sigmoid, [bo], [bo])
            elif kind in ("wd0", "wd1"):
                ACT(S, o_[0:M, 0:n], o_[0:M, 0:n], AF.Tanh, [bo], [bo])
            nchunk += 1
            if nxt and ci % 8 == 3:
                prep_sub(*nxt.pop(0))
        while nxt:
            prep_sub(*nxt.pop(0))
        for d in range(2):
            for hc in range(8):
                for which in range(2):
                    pq = pl[nl % 2]
                    bq = bpl[nl % 2]
                    o_ = ob[nl % 3]
                    bo = bob[nl % 3]
                    if which == 0:
                        MM(S, pq[:, 0:n], wup[d][0:64, hc * 128:(hc + 1) * 128],
                           lora["wd%d" % d][0:64, 0:n], True, True, [bsm, blora], [bq])
                        ACT(S, o_[:, 0:n], pq[:, 0:n], AF.Sigmoid, [bq, g.b_pers], [bo], bias=w0c[d][:, hc:hc + 1])
                        TS(S, "dve", o_[:, 0:n], o_[:, 0:n], -0.6065306597126334, None, ALU.mult, None, [bo], [bo])
                        DMA(S, "pool", LW[d, hc * 128:(hc + 1) * 128, t0:t0 + n], o_[:, 0:n], [bo], [g.sb["LOGW"]])
                    else:
                        MM(S, pq[:, 0:n], aup[d][0:64, hc * 128:(hc + 1) * 128],
                           lora["ad%d" % d][0:64, 0:n], True, True, [bsm, blora], [bq])
                        ACT(S, o_[:, 0:n], pq[:, 0:n], AF.Sigmoid, [bq, g.b_pers], [bo], bias=a0c[d][:, hc:hc + 1])
                        DMA(S, "pool", AAs[d, hc * 128:(hc + 1) * 128, t0:t0 + n], o_[:, 0:n], [bo], [g.sb["AA"]])
                    nl += 1
        if OWN0 <= t0 < OWN1:
            for hc in range(8):
                pq = pl[nl % 2]
                bq = bpl[nl % 2]
                o_ = ob[nl % 3]
                bo = bob[nl % 3]
                MM(S, pq[:, 0:n], gup0[:, hc * 128:(hc + 1) * 128], lora["gd0"][:, 0:n],
                   True, False, [bsm, blora], [bq])
                MM(S, pq[:, 0:n], gup1[0:32, hc * 128:(hc + 1) * 128],
                   lora["gd1"][0:32, 0:n], False, True, [bsm, blora], [bq])
                CP(S, "act", o_[:, 0:n], pq[:, 0:n], [bq], [bo])
                DMA(S, "pool", GG[hc * 128:(hc + 1) * 128, t0 - OWN0:t0 - OWN0 + n], o_[:, 0:n], [bo], [g.sb["GG"]])
                nl += 1


def build(stop_after=99, dbg=()):
    nc = bass.Bass("TRN2", target_bir_lowering=False)
    g = G()
    g.nc = nc
    g.dbg = set(dbg)
    g.scr = {}
    g.sb = {}
    g.S = Sched(nc)
    I = {}

    def inp(name, shape):
        I[name] = nc.dram_tensor(name, list(shape), F32, kind="ExternalInput").ap()

    inp("x", [4096, D]); inp("ctx", [256, D]); inp("c", [D]); inp("c_ctx", [D])
    inp("mod_w", [D, 6 * D]); inp("mod_b", [6 * D]); inp("norm_mix_g", [D]); inp("w_in", [D, NIN])
    inp("shift_mu", [2, 3488]); inp("rwkv_w0", [2, 1024]); inp("rwkv_w_up", [2, 64, 1024])
    inp("rwkv_a0", [2, 1024]); inp("rwkv_a_up", [2, 64, 1024]); inp("rwkv_g_up", [160, 1024])
    inp("rwkv_k_k", [1024]); inp("rwkv_k_a", [1024]); inp("rwkv_r_k", [1024]); inp("lnx_w", [1024])
    inp("lnx_b", [1024]); inp("s5q_are", [128, 64]); inp("s5q_aim", [128, 64]); inp("s5q_lst", [128, 64])
    inp("s5q_bre", [128, 1024]); inp("s5q_bim", [128, 1024]); inp("s5_c_re", [2, 64, 16, 64])
    inp("s5_c_im", [2, 64, 16, 64]); inp("s5_d", [1024]); inp("s5_glu_w", [1024, 1024]); inp("s5_glu_b", [1024])
    inp("w_out", [D, D]); inp("norm_ffn_g", [D]); inp("ffn_w_up", [D, 2 * DFF]); inp("ffn_conv_w", [3, DFF])
    inp("ffn_conv_b", [DFF]); inp("ffn_w_down", [DFF, D]); inp("final_norm_g", [D])
    inp("consts", [128, NCONST])
    g.inp = I
    g.out = nc.dram_tensor("out", [NOWN, D], F32, kind="ExternalOutput").ap()

    AW = 43008
    arena_t = nc.alloc_sbuf_tensor("arena", [128, AW], F32)
    pers_t = nc.alloc_sbuf_tensor("pers", [128, 4096], F32)
    g.arena = Arena(arena_t[:, :], AW, g.S)
    g.pers = Arena(pers_t[:, :], 4096)
    g.ps = nc.alloc_psum_tensor("ps", [128, 4096], F32)[:, :]
    g.b_pers = Buf("pers")
    g.b_const = Buf("const")
    g.b_cvs = Buf()
    g.b_cvp = Buf()
    S = g.S
    cst = g.pers.f32(NCONST)
    DMA(S, "sp", cst, I["consts"], [], [g.b_const])
    g.cst = cst
    g.ident = cst[:, 0:128]
    g.cv_stage = g.pers.f32(128)
    idb = g.pers.bf16(128)
    CP(S, "dve", idb, g.ident, [g.b_const], [g.b_const])
    g.identb = idb

    stage_mod(g)
    if stop_after >= 1:
        stage_proj(g)
    if stop_after >= 2 and not SKIP_RWKV:
        stage_rwkv(g)
    if stop_after >= 3:
        stage_s5(g)
    if stop_after >= 4:
        stage_tail(g)
    if DEBUG and "modx" in g.dbg:
        mo = nc.dram_tensor("modx_o", [128, 96], F32, kind="ExternalOutput").ap()
        DMA(S, "sp", mo, g.modx, [g.b_pers], [Buf()])
    if stop_after < 99:
        z = g.pers.f32(8)
        S.op("dve", lambda e: e.memset(z, 0.0), [], [g.b_pers])
        DMA(S, "sp", g.out[0:128, 0:8], z, [g.b_pers], [Buf()])
    S.finish()
    return nc


NCONST = 968


def make_consts():
    c = np.zeros((128, NCONST), np.float32)
    c[:, 0:128] = np.eye(128, dtype=np.float32)
    i = np.arange(64)
    for d, off in ((0, 128), (1, 448)):
        if d == 0:
            ms = (i[None, :] > i[:, None]); mi = (i[None, :] >= i[:, None])
        else:
            ms = (i[None, :] < i[:, None]); mi = (i[None, :] <= i[:, None])
        m5 = np.concatenate([ms, mi, ms, mi, ms.T], axis=1).astype(np.float32)
        c[0:64, off:off + 320] = m5
        c[64:128, off:off + 320] = m5
    c[:, 768:832] = 1.0
    c[:, 832:896] = 1.0 / 64.0
    p = np.arange(128)
    c[:, 896:904] = (p[:, None] // 16 == np.arange(8)[None, :]).astype(np.float32)
    c[0:64, 904:968] = np.eye(64, dtype=np.float32)
    c[64:128, 904:968] = np.eye(64, dtype=np.float32)
    return c


def core_inputs(inputs, b, j):
    f = lambda a: np.ascontiguousarray(np.asarray(a, dtype=np.float32))
    rev = (j == 1)
    sw = (lambda a: a[::-1]) if rev else (lambda a: a)
    m = {}
    x = inputs["x"][b]
    ctx = inputs["ctx"][b]
    m["x"] = f(x[::-1] if rev else x)
    m["ctx"] = f(ctx[::-1] if rev else ctx)
    m["c"] = f(inputs["c"][b])
    m["c_ctx"] = f(inputs["c_ctx"])
    m["mod_w"] = f(inputs["mod_w"][0]); m["mod_b"] = f(inputs["mod_b"][0])
    m["norm_mix_g"] = f(inputs["norm_mix_g"][0])
    w_in = np.asarray(inputs["w_in"][0])
    mu = np.asarray(inputs["shift_mu"][0])
    if rev:
        perm = np.arange(NIN)
        perm[3232:3296], perm[3296:3360] = np.arange(3296, 3360), np.arange(3232, 3296)
        perm[3360:3424], perm[3424:3488] = np.arange(3424, 3488), np.arange(3360, 3424)
        w_in = w_in[:, perm]
        mu = mu[::-1][:, perm[:3488]]
    m["w_in"] = f(w_in)
    m["shift_mu"] = f(mu)
    for k in ("rwkv_w0", "rwkv_w_up", "rwkv_a0", "rwkv_a_up", "s5_c_re", "s5_c_im"):
        m[k] = f(sw(np.asarray(inputs[k][0])))
    for k, nm in (("s5_a_re", "s5q_are"), ("s5_a_im", "s5q_aim")):
        a_ = sw(np.asarray(inputs[k][0])).reshape(2, 8, 8, 4, 16)
        m[nm] = f(a_.transpose(2, 4, 1, 0, 3).reshape(128, 64))
    ls = sw(np.asarray(inputs["s5_log_step"][0])).reshape(2, 8, 8)
    ls = np.broadcast_to(ls.transpose(2, 1, 0)[:, None, :, :, None], (8, 16, 8, 2, 4))
    m["s5q_lst"] = f(ls.reshape(128, 64))
    for k, nm in (("s5_b_re", "s5q_bre"), ("s5_b_im", "s5q_bim")):
        b_ = sw(np.asarray(inputs[k][0])).reshape(2, 8, 8, 4, 16, 16)
        m[nm] = f(b_.transpose(2, 4, 1, 0, 3, 5).reshape(128, 1024))
    for k in ("rwkv_g_up", "rwkv_k_k", "rwkv_k_a", "lnx_w", "lnx_b", "s5_d", "s5_glu_w", "s5_glu_b", "w_out",
              "norm_ffn_g", "ffn_w_up", "ffn_conv_b", "ffn_w_down"):
        m[k] = f(inputs[k][0])
    m["rwkv_r_k"] = f(np.asarray(inputs["rwkv_r_k"][0]).reshape(1024))
    cw = np.asarray(inputs["ffn_conv_w"][0])
    m["ffn_conv_w"] = f(cw[::-1] if rev else cw)
    m["final_norm_g"] = f(inputs["final_norm_g"])
    m["consts"] = make_consts()
    return m


def kernel(**inputs):
    nc = build()
    in_maps = [core_inputs(inputs, c // 2, c % 2) for c in range(8)]
    res = run_bass_kernel_spmd(nc, in_maps, core_ids=list(range(8)))
    out = np.zeros((4, 4096, D), np.float32)
    for c in range(8):
        b, j = c // 2, c % 2
        o = np.asarray(res.results[c]["out"])
        if j == 0:
            out[b, 0:2048] = o
        else:
            out[b, 2048:4096] = o[::-1]
    return out


def colvec_w(g, dram2d, rows, width, dst):
    S = g.S
    st = g.cv_stage
    DMA(S, "sp", st[0:rows, 0:width], dram2d, [], [g.b_cvs])
    TR(S, g.ps[0:width, 3584:3584 + rows], st[0:rows, 0:width], g.ident[0:rows, 0:rows],
       [g.b_cvs, g.b_const], [g.b_cvp])
    CP(S, "dve", dst, g.ps[0:width, 3584:3584 + rows], [g.b_cvp], [g.b_pers])


def stage_rwkv(g):
    S, nc, A = g.S, g.nc, g.arena
    A.reset()
    I = g.inp
    P, LW, AAs, GG = g.scr["P"], g.scr["LOGW"], g.scr["AA"], g.scr["GG"]
    OT = scratch(g, "OT", [2048, NOWN], BF16)
    LN = (slice(0, 64), slice(64, 128))
    ones = g.cst[:, 768:832]
    ones64 = g.cst[:, 832:896]
    id2 = g.cst[:, 904:968]
    M5 = [g.cst[:, 128:448], g.cst[:, 448:768]]
    kkc, kac, omka, rkc, lwc, lbc = (g.pers.f32(8) for _ in range(6))
    for dst, nm in ((kkc, "rwkv_k_k"), (kac, "rwkv_k_a"), (rkc, "rwkv_r_k"), (lwc, "lnx_w"), (lbc, "lnx_b")):
        colvec(g, I[nm].rearrange("(h p) -> h p", p=128), 8, dst)
    TS(S, "dve", omka, kac, -1.0, 1.0, ALU.mult, ALU.add, [g.b_pers], [g.b_pers])
    bk = [Buf("bank%d" % i) for i in range(8)]
    bank = [g.ps[:, 512 * i:512 * (i + 1)] for i in range(8)]

    def MM2(out, lhsT, rhs, start, stop, reads, writes):
        for ln in LN:
            MM(S, out[ln], lhsT[ln], rhs[ln], start, stop, reads, writes)

    id2b = g.pers.bf16(64)
    CP(S, "dve", id2b, id2, [g.b_const], [g.b_const])

    def TR2(out, in_, reads, writes, lowp=False):
        idm = id2b if (lowp and RWKV_BF16) else id2
        for ln in LN:
            MM(S, out[ln], in_[ln], idm[ln], True, True, reads, writes)

    def t64(n=512, nb=1):
        return [A.f32(n) for _ in range(nb)]

    YH = t64(NOWN, 2)
    bYH = [Buf(), Buf()]
    Rt, Kt, Vt, At, Lt = t64(512, 2), t64(512, 2), t64(512, 2), t64(512, 2), t64(512, 2)
    bin_ = [Buf(), Buf()]
    kk, sq, rin, kap = t64()[0], t64()[0], t64()[0], t64()[0]
    bprep = Buf()
    tt_, kd, bb, tmp, E1, E2, E3, E4 = (t64()[0] for _ in range(8))
    lp = A.bf16 if RWKV_BF16 else A.f32
    AR, BK = lp(1024), lp(1024)
    BH, KH = lp(512), lp(512)
    bop = Buf()
    LWtm, BHtm, KHtm, Vtm = t64()[0], lp(512), lp(512), lp(512)
    btm = Buf()
    X = lp(1024)
    bX = Buf()
    A5s = lp(8 * 320)
    bA5 = Buf()
    PPb = [lp(512), lp(512)]
    bPP = [Buf(), Buf()]
    QT, YL, PHI, ZZ = t64(512, 2), t64(512, 2), t64(512, 2), t64(512, 2)
    bpost = [Buf(), Buf()]
    Sst = t64(64, 2)
    bS = Buf()
    yh, cen, rstd, rk, Gt = t64()[0], t64()[0], t64()[0], t64()[0], t64()[0]
    ob = A.bf16(512)
    bout = Buf()
    nio = 0
    npost = 0
    for hp in range(8):
        r0, k0, v0 = hp * 128, 1024 + hp * 128, 2048 + hp * 128
        hc = slice(hp, hp + 1)
        for d in range(2):
            order = [0, 1, 2, 3, 4] if d == 0 else [0, 8, 7, 6, 5, 4, 3, 2, 1]
            S.op("dve", lambda e, s_=Sst[0]: e.memset(s_, 0.0), [], [bS])
            scur = 0
            for ti in order:
                t0, n = TILES[ti]
                nch = n // 64
                own = OWN0 <= t0 < OWN1
                ib = nio % 2
                nio += 1
                R, Kx, V, AAx, L = Rt[ib], Kt[ib], Vt[ib], At[ib], Lt[ib]
                for qi_, (dst, src) in enumerate(((R, P[r0:r0 + 128, t0:t0 + n]), (Kx, P[k0:k0 + 128, t0:t0 + n]),
                                                  (V, P[v0:v0 + 128, t0:t0 + n]), (AAx, AAs[d, r0:r0 + 128, t0:t0 + n]),
                                                  (L, LW[d, r0:r0 + 128, t0:t0 + n]))):
                    DMA(S, "sp" if qi_ % 2 == 0 else "pool", dst[:, 0:n], src,
                        [g.sb["P"], g.sb["AA"], g.sb["LOGW"]], [bin_[ib]])
                bi = bin_[ib]
                sl = slice(0, n)
                TS(S, "dve", kk[:, sl], Kx[:, sl], kkc[:, hc], None, ALU.mult, None, [bi, g.b_pers], [bprep])
                ACT(S, sq[:, sl], kk[:, sl], AF.Square, [bprep], [bprep])
                MM2(bank[0][:, sl], ones, sq[:, sl], True, True, [bprep, g.b_const], [bk[0]])
                ACT(S, rin[:, sl], bank[0][:, sl], AF.Sqrt, [bk[0]], [bprep])
                TS(S, "dve", rin[:, sl], rin[:, sl], 1e-12, None, ALU.max, None, [bprep], [bprep])
                S.op("dve", lambda e, o=rin[:, sl]: e.reciprocal(o, o), [bprep], [bprep])
                TT(S, "dve", kap[:, sl], kk[:, sl], rin[:, sl], ALU.mult, [bprep], [bprep])
                TS(S, "dve", tt_[:, sl], AAx[:, sl], kac[:, hc], omka[:, hc], ALU.mult, ALU.add,
                   [bi, g.b_pers], [bop])
                TT(S, "pool", kd[:, sl], Kx[:, sl], tt_[:, sl], ALU.mult, [bi, bop], [bop])
                TT(S, "pool", bb[:, sl], kap[:, sl], AAx[:, sl], ALU.mult, [bi, bprep, bop], [bop])
                for c in range(nch):
                    TR2(bank[1][:, c * 64:(c + 1) * 64], L[:, c * 64:(c + 1) * 64], [bi, g.b_const], [bk[1]])
                CP(S, "act", LWtm[:, sl], bank[1][:, sl], [bk[1]], [btm])
                for c in range(nch):
                    MM2(bank[0][:, c * 64:(c + 1) * 64], LWtm[:, c * 64:(c + 1) * 64], M5[d][:, 64:128],
                        True, True, [btm, g.b_const], [bk[0]])
                Gp = bank[0][:, sl]
                ACT(S, E1[:, sl], Gp, AF.Exp, [bk[0]], [bop])
                ACT(S, E3[:, sl], Gp, AF.Exp, [bk[0]], [bop], scale=-1.0)
                TT(S, "dve", tmp[:, sl], Gp, L[:, sl], ALU.subtract, [bk[0], bi], [bop])
                ACT(S, E2[:, sl], tmp[:, sl], AF.Exp, [bop], [bop])
                e13 = E1[:, sl].rearrange("p (c t) -> p c t", t=64)
                gcv = e13[:, :, 63] if d == 0 else e13[:, :, 0]
                e33 = E3[:, sl].rearrange("p (c t) -> p c t", t=64)
                e43 = E4[:, sl].rearrange("p (c t) -> p c t", t=64)
                TT(S, "dve", e43, e33, gcv.unsqueeze(2).broadcast_to([128, nch, 64]), ALU.mult, [bop], [bop])
                AR4 = AR[:, 0:2 * n].rearrange("p (c w t) -> p c w t", w=2, t=64)
                BK4 = BK[:, 0:2 * n].rearrange("p (c w t) -> p c w t", w=2, t=64)
                v3 = lambda a: a[:, sl].rearrange("p (c t) -> p c t", t=64)
                STT(S, "dve", AR4[:, :, 0, :], v3(kap), -1.0, v3(E2), ALU.mult, ALU.mult, [bprep, bop], [bop])
                TT(S, "pool", AR4[:, :, 1, :], v3(R), v3(E1), ALU.mult, [bi, bop], [bop])
                TT(S, "pool", BK4[:, :, 0, :], v3(bb), v3(E3), ALU.mult, [bop], [bop])
                TT(S, "dve", BK4[:, :, 1, :], v3(kd), v3(E3), ALU.mult, [bop], [bop])
                TT(S, "pool", BH[:, sl], bb[:, sl], E4[:, sl], ALU.mult, [bop], [bop])
                TT(S, "dve", KH[:, sl], kd[:, sl], E4[:, sl], ALU.mult, [bop], [bop])
                X3 = X[:, 0:2 * n].rearrange("p (c w) -> p c w", w=128)
                for c in range(nch):
                    TR2(bank[1][:, c * 64:(c + 1) * 64], AR4[:, c, 0, :], [bop, g.b_const], [bk[1]], lowp=True)
                CP(S, "act", X3[:, :, 0:64], bank[1][:, sl].rearrange("p (c t) -> p c t", t=64), [bk[1]], [bX])
                for src_, dst_ in ((BH, BHtm), (KH, KHtm), (V, Vtm)):
                    for c in range(nch):
                        TR2(bank[1][:, c * 64:(c + 1) * 64], src_[:, c * 64:(c + 1) * 64], [bop, bi, g.b_const], [bk[1]],
                            lowp=src_ is not V)
                    CP(S, "act" if dst_ is not KHtm else "dve", dst_[:, sl], bank[1][:, sl], [bk[1]], [btm])
                A53 = A5s[:, 0:nch * 320].rearrange("p (c w) -> p c w", w=320)
                for c in range(nch):
                    pb = bank[2 + c % 2]
                    bpb = bk[2 + c % 2]
                    arc = AR[:, c * 128:(c + 1) * 128]
                    MM2(pb[:, 0:128], BK4[:, c, 0, :], arc, True, True, [bop], [bpb])
                    MM2(pb[:, 128:256], BK4[:, c, 1, :], arc, True, True, [bop], [bpb])
                    MM2(pb[:, 256:320], AR4[:, c, 0, :], BK4[:, c, 0, :], True, True, [bop], [bpb])
                    TT(S, "dve", A53[:, c, :], pb[:, 0:320], M5[d], ALU.mult, [bpb, g.b_const], [bA5])
                for c in range(nch):
                    MM2(bank[4][:, c * 64:(c + 1) * 64], A53[:, c, 128:192], Vtm[:, c * 64:(c + 1) * 64],
                        True, True, [bA5, btm], [bk[4]])
                CP(S, "act", X3[:, :, 64:128], bank[4][:, sl].rearrange("p (c t) -> p c t", t=64), [bk[4]], [bX])
                for g0 in range(0, nch, 4):
                    gn = min(4, nch - g0)
                    for lvl in range(6):
                        for c in range(g0, g0 + gn):
                            if lvl == 0:
                                Pm, PTm = A53[:, c, 256:320], A53[:, c, 0:64]
                                rd = [bA5]
                            else:
                                pp_ = PPb[(lvl - 1) % 2]
                                Pm = pp_[:, (c - g0) * 128:(c - g0) * 128 + 64]
                                PTm = pp_[:, (c - g0) * 128 + 64:(c - g0) * 128 + 128]
                                rd = [bPP[(lvl - 1) % 2]]
                            MM2(bank[4][:, (c - g0) * 128:(c - g0 + 1) * 128], PTm, X3[:, c, :], True, True,
                                rd + [bX], [bk[4]])
                            if lvl < 5:
                                MM2(bank[5][:, (c - g0) * 128:(c - g0) * 128 + 64], PTm, Pm, True, True, rd, [bk[5]])
                                MM2(bank[5][:, (c - g0) * 128 + 64:(c - g0 + 1) * 128], Pm, PTm, True, True, rd,
                                    [bk[5]])
                        xs = X[:, g0 * 128:(g0 + gn) * 128]
                        TT(S, "dve", xs, xs, bank[4][:, 0:gn * 128], ALU.add, [bk[4], bX], [bX])
                        if lvl < 5:
                            CP(S, "act", PPb[lvl % 2][:, 0:gn * 128], bank[5][:, 0:gn * 128], [bk[5]], [bPP[lvl % 2]])
                pbi = npost % 2
                npost += 1
                bpo = bpost[pbi]
                e1_or = AR4[:, :, 1, :]
                for c in range(nch):
                    W1, W2 = X3[:, c, 0:64], X3[:, c, 64:128]
                    cs = slice(c * 64, (c + 1) * 64)
                    if own:
                        MM2(bank[2][:, cs], W1, A53[:, c, 64:128], True, True, [bX, bA5], [bk[2]])
                        MM2(bank[3][:, cs], W2, A53[:, c, 64:128], True, False, [bX, bA5], [bk[3]])
                        MM2(bank[3][:, cs], Vtm[:, cs], A53[:, c, 192:256], False, True, [btm, bA5], [bk[3]])
                    MM2(bank[6][:, cs], W1, BHtm[:, cs], True, True, [bX, btm], [bk[6]])
                    MM2(bank[7][:, cs], BHtm[:, cs], W2, True, False, [bX, btm], [bk[7]])
                    MM2(bank[7][:, cs], KHtm[:, cs], Vtm[:, cs], False, True, [btm], [bk[7]])
                if own:
                    TT(S, "dve", QT[pbi][:, sl].rearrange("p (c t) -> p c t", t=64),
                       bank[2][:, sl].rearrange("p (c t) -> p c t", t=64), e1_or, ALU.add, [bk[2], bop], [bpo])
                    CP(S, "act", YL[pbi][:, sl], bank[3][:, sl], [bk[3]], [bpo])
                ph3 = PHI[pbi][:, sl].rearrange("p (c t) -> p c t", t=64)
                TT(S, "pool", ph3, id2.unsqueeze(1).broadcast_to([128, nch, 64]),
                   gcv.unsqueeze(2).broadcast_to([128, nch, 64]), ALU.mult, [g.b_const, bop], [bpo])
                TT(S, "dve", PHI[pbi][:, sl], PHI[pbi][:, sl], bank[6][:, sl], ALU.add, [bk[6], bpo], [bpo])
                CP(S, "act", ZZ[pbi][:, sl], bank[7][:, sl], [bk[7]], [bpo])
                corder = range(nch) if d == 0 else range(nch - 1, -1, -1)
                for c in corder:
                    cs = slice(c * 64, (c + 1) * 64)
                    Sc, Sn = Sst[scur], Sst[1 - scur]
                    if own:
                        yo = t0 - OWN0 + c * 64
                        MM2(bank[1][:, 0:64], Sc, QT[pbi][:, cs], True, True, [bS, bpo], [bk[1]])
                        TT(S, "dve", YH[d][:, yo:yo + 64], bank[1][:, 0:64], YL[pbi][:, cs], ALU.add,
                           [bk[1], bpo], [bYH[d]])
                    MM2(bank[1][:, 64:128], PHI[pbi][:, cs], Sc, True, True, [bS, bpo], [bk[1]])
                    TT(S, "dve", Sn, bank[1][:, 64:128], ZZ[pbi][:, cs], ALU.add, [bk[1], bpo], [bS])
                    scur = 1 - scur
        for ti in range(1, 5):
            t0, n = TILES[ti]
            o0 = t0 - OWN0
            ib = nio % 2
            nio += 1
            R, Kx, V = Rt[ib], Kt[ib], Vt[ib]
            for dst, src in ((R, P[r0:r0 + 128, t0:t0 + n]), (Kx, P[k0:k0 + 128, t0:t0 + n]),
                             (V, P[v0:v0 + 128, t0:t0 + n])):
                DMA(S, "sp", dst[:, 0:n], src, [g.sb["P"]], [bin_[ib]])
            DMA(S, "sp", Gt[:, 0:n], GG[r0:r0 + 128, o0:o0 + n], [g.sb["GG"]], [bout])
            bi = bin_[ib]
            TT(S, "pool", yh, YH[0][:, o0:o0 + n], YH[1][:, o0:o0 + n], ALU.add, [bYH[0], bYH[1]], [bout])
            MM2(bank[0], ones64, yh, True, True, [bout, g.b_const], [bk[0]])
            TT(S, "dve", cen, yh, bank[0], ALU.subtract, [bk[0], bout], [bout])
            ACT(S, yh, cen, AF.Square, [bout], [bout])
            MM2(bank[0], ones64, yh, True, True, [bout, g.b_const], [bk[0]])
            TS(S, "dve", rstd, bank[0], 64e-5, None, ALU.add, None, [bk[0]], [bout])
            ACT(S, rstd, rstd, AF.Sqrt, [bout], [bout])
            S.op("dve", lambda e, o=rstd: e.reciprocal(o, o), [bout], [bout])
            TT(S, "dve", cen, cen, rstd, ALU.mult, [bout], [bout])
            TS(S, "dve", cen, cen, lwc[:, hc], lbc[:, hc], ALU.mult, ALU.add, [bout, g.b_pers], [bout])
            STT(S, "dve", rk, R, rkc[:, hc], Kx, ALU.mult, ALU.mult, [bi, g.b_pers], [bout])
            MM2(bank[0], ones, rk, True, True, [bout, g.b_const], [bk[0]])
            TT(S, "dve", rk, bank[0], V, ALU.mult, [bk[0], bi], [bout])
            TT(S, "pool", cen, cen, rk, ALU.add, [bout], [bout])
            TT(S, "dve", ob, cen, Gt, ALU.mult, [bout], [bout])
            DMA(S, "pool", OT[r0:r0 + 128, o0:o0 + n], ob, [bout], [g.sb["OT"]])


def stage_s5(g):
    S, nc, A = g.S, g.nc, g.arena
    A.reset()
    I = g.inp
    P = g.scr["P"]
    ZT = scratch(g, "ZT", [1024, NOWN], F32)
    ident = g.ident
    bdm = g.cst[:, 896:904]
    MB = S5_MB
    NCTX, JB0, JB1 = 256 // MB, OWN0 // MB, OWN1 // MB
    bk = [Buf("bank%d" % i) for i in range(8)]
    bank = [g.ps[:, 512 * i:512 * (i + 1)] for i in range(8)]
    bpar = Buf("s5par")
    NQ = 64

    def tl(n=NQ):
        return A.f32(n)

    are, aim, lst = tl(), tl(), tl()
    v4 = lambda a: a.rearrange("p (gs d pb) -> p gs d pb", gs=8, d=2, pb=4)
    DMA(S, "sp", are, I["s5q_are"], [], [bpar])
    DMA(S, "sp", aim, I["s5q_aim"], [], [bpar])
    DMA(S, "sp", lst, I["s5q_lst"], [], [bpar])
    dt_, mag, ang, cc, ss, t1, t2, t3 = (tl() for _ in range(8))
    ACT(S, dt_, lst, AF.Exp, [bpar], [bpar])
    TT(S, "dve", mag, are, dt_, ALU.mult, [bpar], [bpar])
    ACT(S, mag, mag, AF.Exp, [bpar], [bpar])
    TT(S, "dve", ang, aim, dt_, ALU.mult, [bpar], [bpar])
    hp = g.pers.f32(8)
    S.op("dve", lambda e: e.memset(hp, 1.5707963267948966), [], [g.b_pers])
    ACT(S, ss, ang, AF.Sin, [bpar], [bpar], scale=1.0 / 16)
    ACT(S, cc, ang, AF.Sin, [bpar, g.b_pers], [bpar], scale=1.0 / 16, bias=hp[:, 0:1])

    def csq(cr, ci):
        TT(S, "dve", t1, cr, cr, ALU.mult, [bpar], [bpar])
        TT(S, "dve", t2, ci, ci, ALU.mult, [bpar], [bpar])
        TT(S, "dve", t3, cr, ci, ALU.mult, [bpar], [bpar])
        TT(S, "dve", cr, t1, t2, ALU.subtract, [bpar], [bpar])
        TS(S, "dve", ci, t3, 2.0, None, ALU.mult, None, [bpar], [bpar])

    for _ in range(4):
        csq(cc, ss)
    lr, li, nli = tl(), tl(), tl()
    TT(S, "dve", lr, mag, cc, ALU.mult, [bpar], [bpar])
    TT(S, "dve", li, mag, ss, ALU.mult, [bpar], [bpar])
    TS(S, "dve", nli, li, -1.0, None, ALU.mult, None, [bpar], [bpar])
    fr, fi, den, nr = tl(), tl(), tl(), tl()
    TT(S, "dve", t1, are, are, ALU.mult, [bpar], [bpar])
    TT(S, "dve", t2, aim, aim, ALU.mult, [bpar], [bpar])
    TT(S, "dve", den, t1, t2, ALU.add, [bpar], [bpar])
    S.op("dve", lambda e: e.reciprocal(den, den), [bpar], [bpar])
    TS(S, "dve", nr, lr, -1.0, None, ALU.add, None, [bpar], [bpar])
    TT(S, "dve", t1, nr, are, ALU.mult, [bpar], [bpar])
    TT(S, "dve", t2, li, aim, ALU.mult, [bpar], [bpar])
    TT(S, "dve", t1, t1, t2, ALU.add, [bpar], [bpar])
    TT(S, "dve", fr, t1, den, ALU.mult, [bpar], [bpar])
    TT(S, "dve", t1, li, are, ALU.mult, [bpar], [bpar])
    TT(S, "dve", t2, nr, aim, ALU.mult, [bpar], [bpar])
    TT(S, "dve", t1, t1, t2, ALU.subtract, [bpar], [bpar])
    TT(S, "dve", fi, t1, den, ALU.mult, [bpar], [bpar])
    TBr, TBi, nTBi = A.f32(MB * NQ), A.f32(MB * NQ), A.f32(MB * NQ)
    tb = lambda a, i: a[:, i * NQ:(i + 1) * NQ]
    CP(S, "dve", tb(TBr, 0), lr, [bpar], [bpar])
    CP(S, "dve", tb(TBi, 0), li, [bpar], [bpar])
    for i in range(1, MB):
        TT(S, "dve", t1, tb(TBr, i - 1), lr, ALU.mult, [bpar], [bpar])
        TT(S, "dve", t2, tb(TBi, i - 1), li, ALU.mult, [bpar], [bpar])
        TT(S, "dve", tb(TBr, i), t1, t2, ALU.subtract, [bpar], [bpar])
        TT(S, "dve", t1, tb(TBr, i - 1), li, ALU.mult, [bpar], [bpar])
        TT(S, "dve", t2, tb(TBi, i - 1), lr, ALU.mult, [bpar], [bpar])
        TT(S, "dve", tb(TBi, i), t1, t2, ALU.add, [bpar], [bpar])
    TS(S, "dve", nTBi, TBi, -1.0, None, ALU.mult, None, [bpar], [bpar])
    NL = 10
    LMr, LMi, nLMi = A.f32(NL * NQ), A.f32(NL * NQ), A.f32(NL * NQ)
    CP(S, "dve", tb(LMr, 0), tb(TBr, MB - 1), [bpar], [bpar])
    CP(S, "dve", tb(LMi, 0), tb(TBi, MB - 1), [bpar], [bpar])
    for k in range(1, NL):
        CP(S, "dve", tb(LMr, k), tb(LMr, k - 1), [bpar], [bpar])
        CP(S, "dve", tb(LMi, k), tb(LMi, k - 1), [bpar], [bpar])
        csq(tb(LMr, k), tb(LMi, k))
    TS(S, "dve", nLMi, LMi, -1.0, None, ALU.mult, None, [bpar], [bpar])
    Bq = [A.f32(NQ * 16) for _ in range(2)]
    DMA(S, "sp", Bq[0], I["s5q_bre"], [], [bpar])
    DMA(S, "sp", Bq[1], I["s5q_bim"], [], [bpar])
    Bbr, Bbi, tq = A.f32(NQ * 16), A.f32(NQ * 16), A.f32(NQ * 16)
    q3 = lambda a: a.rearrange("p (c h) -> p c h", h=16)
    bc = lambda a: a.unsqueeze(2).broadcast_to([128, NQ, 16])
    TT(S, "dve", q3(Bbr), q3(Bq[0]), bc(fr), ALU.mult, [bpar], [bpar])
    TT(S, "dve", q3(tq), q3(Bq[1]), bc(fi), ALU.mult, [bpar], [bpar])
    TT(S, "dve", Bbr, Bbr, tq, ALU.subtract, [bpar], [bpar])
    TT(S, "dve", q3(Bbi), q3(Bq[1]), bc(fr), ALU.mult, [bpar], [bpar])
    TT(S, "dve", q3(tq), q3(Bq[0]), bc(fi), ALU.mult, [bpar], [bpar])
    TT(S, "dve", Bbi, Bbi, tq, ALU.add, [bpar], [bpar])
    sdc = g.pers.f32(8)
    colvec(g, I["s5_d"].rearrange("(k p) -> k p", p=128), 8, sdc)

    u = A.f32(T)
    bu_ = Buf()
    sre2, sim2 = [A.f32(T), A.f32(T)], [A.f32(T), A.f32(T)]
    bs2 = [Buf(), Buf()]
    bRe2, bIm2 = [[Buf(), Buf()], [Buf(), Buf()]], [[Buf(), Buf()], [Buf(), Buf()]]
    nunit = 0
    Cn = [A.f32(64) for _ in range(2)]
    bCn = Buf()
    bdrow = A.f32(128)
    bbd = Buf()
    BDB2 = [[[A.f32(128) for _ in range(2)] for _ in range(4)] for _ in range(2)]
    BDC2 = [[[A.f32(128) for _ in range(2)] for _ in range(4)] for _ in range(2)]
    bBD2 = [Buf(), Buf()]
    Rr2 = [[A.f32(T // MB) for _ in range(2)] for _ in range(2)]
    Ri2 = [[A.f32(T // MB) for _ in range(2)] for _ in range(2)]
    bRr2 = [[Buf(), Buf()], [Buf(), Buf()]]
    bRi2 = [[Buf(), Buf()], [Buf(), Buf()]]
    yt, t5, zt = A.f32(512), A.f32(512), A.f32(512)
    by = Buf()
    for gs in range(8):
        DMA(S, "sp", u, P[3072 + gs * 128:3072 + (gs + 1) * 128, :], [g.sb["P"]], [bu_])
        nacc = 0
        for d in range(2):
            Td = 2304 if d == 0 else T
            nblk = Td // MB
            for w, nm in enumerate(("s5_c_re", "s5_c_im")):
                DMA(S, "sp", Cn[w], I[nm][d, gs * 8:(gs + 1) * 8].rearrange("g h p -> (g h) p"), [], [bCn])
            for pb in range(4):
                qi = (gs * 2 + d) * 4 + pb
                us = nunit % 2
                nunit += 1
                sre, sim, bs = sre2[us], sim2[us], bs2[us]
                bRe, bIm = bRe2[us], bIm2[us]
                ball = bRe + bIm
                BDB, BDC, bBD = BDB2[us], BDC2[us], bBD2[us]
                Rr, Ri, bRr, bRi = Rr2[us], Ri2[us], bRr2[us], bRi2[us]
                for w in range(2):
                    srcB = (Bbr, Bbi)[w][:, qi * 16:(qi + 1) * 16]
                    TT(S, "dve", bdrow.rearrange("p (g h) -> p g h", h=16),
                       srcB.unsqueeze(1).broadcast_to([128, 8, 16]), bdm.unsqueeze(2).broadcast_to([128, 8, 16]),
                       ALU.mult, [bpar, g.b_const], [bbd])
                    TR(S, bank[7][:, 0:128], bdrow, ident, [bbd, g.b_const], [bk[7]])
                    CP(S, "act", BDB[pb][w], bank[7][:, 0:128], [bk[7]], [bBD])
                    srcC = Cn[w][:, pb * 16:(pb + 1) * 16]
                    TT(S, "dve", bdrow.rearrange("p (g h) -> p g h", h=16),
                       srcC.unsqueeze(1).broadcast_to([128, 8, 16]), bdm.unsqueeze(2).broadcast_to([128, 8, 16]),
                       ALU.mult, [bCn, g.b_const], [bbd])
                    TR(S, bank[7][:, 128:256], bdrow, ident, [bbd, g.b_const], [bk[7]])
                    if w == 0:
                        CP(S, "act", BDC[pb][w], bank[7][:, 128:256], [bk[7]], [bBD])
                    else:
                        ACT(S, BDC[pb][w], bank[7][:, 128:256], AF.Copy, [bk[7]], [bBD], scale=-1.0)
                for w, dst in enumerate((sre, sim)):
                    for tt0 in range(0, Td, 512):
                        n = min(512, Td - tt0)
                        pbk = 4 + (tt0 // 512 + w) % 2
                        MM(S, bank[pbk][:, 0:n], BDB[pb][w], u[:, tt0:tt0 + n], True, True, [bBD, bu_], [bk[pbk]])
                        CP(S, "act", dst[:, tt0:tt0 + n], bank[pbk][:, 0:n],
                           [bk[pbk]], [bs] + ball)
                r3 = sre[:, 0:Td].rearrange("p (j i) -> p j i", i=MB)
                i3 = sim[:, 0:Td].rearrange("p (j i) -> p j i", i=MB)
                c_lr, c_li, c_nli = lr[:, qi:qi + 1], li[:, qi:qi + 1], nli[:, qi:qi + 1]
                seq = range(1, MB) if d == 0 else range(MB - 2, -1, -1)
                for i in seq:
                    ip = i - 1 if d == 0 else i + 1
                    wr, wi, pr_, pi2 = bRe[i % 2], bIm[i % 2], bRe[ip % 2], bIm[ip % 2]
                    STT(S, "dve", r3[:, :, i], r3[:, :, ip], c_lr, r3[:, :, i], ALU.mult, ALU.add, [bs, bpar, pr_], [wr])
                    STT(S, "dve", i3[:, :, i], i3[:, :, ip], c_lr, i3[:, :, i], ALU.mult, ALU.add, [bs, bpar, pi2], [wi])
                    STT(S, "dve", r3[:, :, i], i3[:, :, ip], c_nli, r3[:, :, i], ALU.mult, ALU.add, [bs, bpar, pi2], [wr])
                    STT(S, "dve", i3[:, :, i], r3[:, :, ip], c_li, i3[:, :, i], ALU.mult, ALU.add, [bs, bpar, pr_], [wi])
                ie = MB - 1 if d == 0 else 0
                cur = 0
                CP(S, "dve", Rr[0][:, 0:nblk], r3[:, :, ie], [bs] + ball, [bRr[0]])
                CP(S, "dve", Ri[0][:, 0:nblk], i3[:, :, ie], [bs] + ball, [bRi[0]])
                segs = [(0, nblk)] if d == 0 else [(0, NCTX), (NCTX, nblk)]
                for si, (lo, hi) in enumerate(segs):
                    if d == 1 and si == 1:
                        a_r, a_i = Rr[cur], Ri[cur]
                        l0r, l0i, nl0i = (tb(z_, 0)[:, qi:qi + 1] for z_ in (LMr, LMi, nLMi))
                        e = hi - 1
                        bR = [bRr[cur], bRi[cur]]
                        STT(S, "dve", a_r[:, e:e + 1], a_r[:, 0:1], l0r, a_r[:, e:e + 1], ALU.mult, ALU.add, bR + [bpar], bR)
                        STT(S, "dve", a_r[:, e:e + 1], a_i[:, 0:1], nl0i, a_r[:, e:e + 1], ALU.mult, ALU.add, bR + [bpar], bR)
                        STT(S, "dve", a_i[:, e:e + 1], a_i[:, 0:1], l0r, a_i[:, e:e + 1], ALU.mult, ALU.add, bR + [bpar], bR)
                        STT(S, "dve", a_i[:, e:e + 1], a_r[:, 0:1], l0i, a_i[:, e:e + 1], ALU.mult, ALU.add, bR + [bpar], bR)
                    k = 0
                    sh = 1
                    while sh < hi - lo:
                        a_r, a_i, n_r, n_i = Rr[cur], Ri[cur], Rr[1 - cur], Ri[1 - cur]
                        pr, pi_, npi = (tb(z_, k)[:, qi:qi + 1] for z_ in (LMr, LMi, nLMi))
                        if d == 0:
                            dst_s, src_s, keep = slice(lo + sh, hi), slice(lo, hi - sh), slice(lo, lo + sh)
                        else:
                            dst_s, src_s, keep = slice(lo, hi - sh), slice(lo + sh, hi), slice(hi - sh, hi)
                        ar_b, ai_b, nr_b, ni_b = bRr[cur], bRi[cur], bRr[1 - cur], bRi[1 - cur]
                        CP(S, "act", n_r[:, 0:nblk], a_r[:, 0:nblk], [ar_b], [nr_b])
                        CP(S, "act", n_i[:, 0:nblk], a_i[:, 0:nblk], [ai_b], [ni_b])
                        STT(S, "dve", n_r[:, dst_s], a_r[:, src_s], pr, n_r[:, dst_s], ALU.mult, ALU.add, [ar_b, bpar], [nr_b])
                        STT(S, "dve", n_i[:, dst_s], a_i[:, src_s], pr, n_i[:, dst_s], ALU.mult, ALU.add, [ai_b, bpar], [ni_b])
                        STT(S, "dve", n_r[:, dst_s], a_i[:, src_s], npi, n_r[:, dst_s], ALU.mult, ALU.add, [ai_b, bpar], [nr_b])
                        STT(S, "dve", n_i[:, dst_s], a_r[:, src_s], pi_, n_i[:, dst_s], ALU.mult, ALU.add, [ar_b, bpar], [ni_b])
                        cur = 1 - cur
                        sh *= 2
                        k += 1
                a_r, a_i = Rr[cur], Ri[cur]
                if d == 0:
                    sin_r, sin_i = a_r[:, JB0 - 1:JB1 - 1], a_i[:, JB0 - 1:JB1 - 1]
                else:
                    sin_r, sin_i = a_r[:, JB0 + 1:JB1 + 1], a_i[:, JB0 + 1:JB1 + 1]
                for i in range(MB):
                    e = i if d == 0 else MB - 1 - i
                    pr, pi_, npi = (z_[:, e * NQ + qi:e * NQ + qi + 1] for z_ in (TBr, TBi, nTBi))
                    bRc = [bRr[cur], bRi[cur]]
                    if i < 2:
                        rdx = ball
                    else:
                        rdx = []
                    STT(S, "dve", r3[:, JB0:JB1, i], sin_r, pr, r3[:, JB0:JB1, i], ALU.mult, ALU.add, rdx + bRc + [bpar], [bRe[i % 2]])
                    STT(S, "dve", i3[:, JB0:JB1, i], sin_i, pr, i3[:, JB0:JB1, i], ALU.mult, ALU.add, rdx + bRc + [bpar], [bIm[i % 2]])
                    STT(S, "dve", r3[:, JB0:JB1, i], sin_i, npi, r3[:, JB0:JB1, i], ALU.mult, ALU.add, bRc + [bpar], [bRe[i % 2]])
                    STT(S, "dve", i3[:, JB0:JB1, i], sin_r, pi_, i3[:, JB0:JB1, i], ALU.mult, ALU.add, bRc + [bpar], [bIm[i % 2]])
                for ti in range(4):
                    osl = slice(OWN0 + ti * 512, OWN0 + (ti + 1) * 512)
                    MM(S, bank[ti], BDC[pb][0], sre[:, osl], nacc == 0, False, [bBD, bs] + ball, [bk[ti]])
                    MM(S, bank[ti], BDC[pb][1], sim[:, osl], False, nacc == 7, [bBD, bs] + ball, [bk[ti]])
                nacc += 1
        for ti in range(4):
            osl = slice(OWN0 + ti * 512, OWN0 + (ti + 1) * 512)
            STT(S, "dve", yt, u[:, osl], sdc[:, gs:gs + 1], bank[ti], ALU.mult, ALU.add, [bu_, bk[ti], g.b_pers], [by])
            gelu_tanh(S, zt, yt, t5, by)
            DMA(S, "pool", ZT[gs * 128:(gs + 1) * 128, ti * 512:(ti + 1) * 512], zt, [by], [g.sb["ZT"]])


def gelu_tanh(S, out, y, tmp, b):
    ACT(S, tmp, y, AF.Square, [b], [b])
    TS(S, "dve", tmp, tmp, 0.044715, 1.0, ALU.mult, ALU.add, [b], [b])
    TT(S, "dve", tmp, tmp, y, ALU.mult, [b], [b])
    ACT(S, tmp, tmp, AF.Sigmoid, [b], [b], scale=1.5957691216057308)
    TT(S, "dve", out, y, tmp, ALU.mult, [b], [b])


def rstd_from(S, x_, st, bx, bst, junk, bjunk, eps=1e-6):
    ACT(S, junk, x_, AF.Square, [bx], [bjunk, bst], accum_out=st[:, 0:1])
    TS(S, "dve", st[:, 1:2], st[:, 0:1], 1.0 / D, eps, ALU.mult, ALU.add, [bst], [bst])
    ACT(S, st[:, 2:3], st[:, 1:2], AF.Sqrt, [bst], [bst])
    S.op("dve", lambda e, o=st[:, 3:4], i_=st[:, 2:3]: e.reciprocal(o, i_), [bst], [bst])


def stage_tail(g):
    S, nc, A = g.S, g.nc, g.arena
    I = g.inp
    OT, ZT = g.scr["OT"], g.scr["ZT"]
    bk = [Buf("bank%d" % i) for i in range(8)]
    bank = [g.ps[:, 512 * i:512 * (i + 1)] for i in range(8)]
    A.reset()
    gw = A.f32(8 * 1024)
    gw3 = gw.rearrange("p (k n) -> p k n", n=1024)
    bgw = Buf()
    DMA(S, "sp", gw3, I["s5_glu_w"].rearrange("(k p) n -> p k n", p=128), [], [bgw])
    gbc = g.pers.f32(8)
    colvec(g, I["s5_glu_b"].rearrange("(k p) -> k p", p=128), 8, gbc)
    zt = [A.f32(8 * 512) for _ in range(2)]
    bz = [Buf(), Buf()]
    gt_ = [A.f32(512) for _ in range(2)]
    og = [A.bf16(512) for _ in range(2)]
    bg = [Buf(), Buf()]
    ZTv = ZT.rearrange("(k p) t -> p k t", p=128)
    n_ = 0
    for ti in range(4):
        z3 = zt[ti % 2].rearrange("p (k t) -> p k t", t=512)
        DMA(S, "sp", z3, ZTv[:, :, ti * 512:(ti + 1) * 512], [g.sb["ZT"]], [bz[ti % 2]])
        for nc_ in range(8):
            pb = n_ % 4
            for kc in range(8):
                MM(S, bank[pb], gw3[:, kc, nc_ * 128:(nc_ + 1) * 128], z3[:, kc, :],
                   kc == 0, kc == 7, [bgw, bz[ti % 2]], [bk[pb]])
            ACT(S, gt_[n_ % 2], bank[pb], AF.Sigmoid, [bk[pb], g.b_pers], [bg[n_ % 2]], bias=gbc[:, nc_:nc_ + 1])
            TT(S, "dve", og[n_ % 2], z3[:, nc_, :], gt_[n_ % 2], ALU.mult, [bz[ti % 2], bg[n_ % 2]], [bg[n_ % 2]])
            DMA(S, "pool", OT[1024 + nc_ * 128:1024 + (nc_ + 1) * 128, ti * 512:(ti + 1) * 512], og[n_ % 2],
                [bg[n_ % 2]], [g.sb["OT"]])
            n_ += 1
    A.reset()
    X1 = scratch(g, "X1", [NOWN, D], F32)
    H2T = scratch(g, "H2T", [16, 128, NOWN], BF16)
    Wo = A.bf16(16 * 2048)
    Wo3 = Wo.rearrange("p (k n) -> p k n", n=2048)
    bWo = Buf()
    wst = [A.f32(2048) for _ in range(2)]
    bwst = [Buf(), Buf()]
    for mc in range(16):
        DMA(S, "sp" if mc % 2 == 0 else "pool", wst[mc % 2], I["w_out"][mc * 128:(mc + 1) * 128, :], [], [bwst[mc % 2]])
        CP(S, "act" if mc % 2 == 0 else "pool", Wo3[:, mc, :], wst[mc % 2], [bwst[mc % 2]], [bWo])
    gtr = A.f32(2048)
    brow = Buf()
    DMA(S, "sp", gtr, g.scr["gates"][0].rearrange("a b -> (a b)").partition_broadcast(128), [g.sb["gates"]], [brow])
    xt = [A.f32(2048) for _ in range(2)]
    bxt = [Buf(), Buf()]
    x1 = [A.f32(2048) for _ in range(2)]
    bx1 = [Buf(), Buf()]
    xn = [A.bf16(2048) for _ in range(2)]
    bxn = [Buf(), Buf()]
    junk = A.bf16(2048)
    bjunk = Buf()
    st4 = [A.f32(8) for _ in range(2)]
    bst = [Buf(), Buf()]
    ot = [A.bf16(16 * 128) for _ in range(2)]
    bot = [Buf(), Buf()]
    h2 = [A.bf16(16 * 128) for _ in range(2)]
    bh2 = [Buf(), Buf()]
    OTv = OT.rearrange("(k p) t -> p k t", p=128)
    H2v = H2T.rearrange("k p t -> p k t")
    ptr = g.ps[:, 2048:3072].bitcast(BF16)
    bptr = Buf()
    for su in range(16):
        i2 = su % 2
        o3 = ot[i2].rearrange("p (k t) -> p k t", t=128)
        DMA(S, "sp", o3, OTv[:, :, su * 128:(su + 1) * 128], [g.sb["OT"]], [bot[i2]])
        DMA(S, "pool", xt[i2], I["x"][su * 128:(su + 1) * 128, :], [], [bxt[i2]])
        for db in range(4):
            for mc in range(16):
                MM(S, bank[db], o3[:, mc, :], Wo3[:, mc, db * 512:(db + 1) * 512], mc == 0, mc == 15,
                   [bot[i2], bWo], [bk[db]])
            dsl = slice(db * 512, (db + 1) * 512)
            TT(S, "dve", x1[i2][:, dsl], bank[db], gtr[:, dsl], ALU.mult, [bk[db], brow], [bx1[i2]])
            TT(S, "pool", x1[i2][:, dsl], x1[i2][:, dsl], xt[i2][:, dsl], ALU.add, [bx1[i2], bxt[i2]], [bx1[i2]])
        DMA(S, "pool", X1[su * 128:(su + 1) * 128, :], x1[i2], [bx1[i2]], [g.sb["X1"]])
        rstd_from(S, x1[i2], st4[i2], bx1[i2], bst[i2], junk, bjunk)
        TS(S, "dve", xn[i2], x1[i2], st4[i2][:, 3:4], None, ALU.mult, None, [bx1[i2], bst[i2]], [bxn[i2]])
        for kc in range(16):
            TR(S, ptr[:, kc * 128:(kc + 1) * 128], xn[i2][:, kc * 128:(kc + 1) * 128], g.identb,
               [bxn[i2], g.b_const], [bptr])
        h3 = h2[i2].rearrange("p (k t) -> p k t", t=128)
        for kc in range(16):
            srcp = ptr[:, kc * 128:(kc + 1) * 128]
            if kc % 2 == 0:
                TS(S, "dve", h3[:, kc, :], srcp, g.A2x[:, kc:kc + 1], g.B2x[:, kc:kc + 1], ALU.mult, ALU.add,
                   [bptr, g.b_pers], [bh2[i2]])
            else:
                ACT(S, h3[:, kc, :], srcp, AF.Identity, [bptr, g.b_pers], [bh2[i2]],
                    bias=g.B2x[:, kc:kc + 1], scale=g.A2x[:, kc:kc + 1])
        DMA(S, "sp", H2v[:, :, su * 128:(su + 1) * 128], h3, [bh2[i2]], [g.sb["H2T"]])
    A.reset()
    cw = [g.pers.f32(44) for _ in range(3)]
    cb = g.pers.f32(44)
    for i in range(3):
        colvec(g, I["ffn_conv_w"][i].rearrange("(k p) -> k p", p=128), 44, cw[i])
    colvec(g, I["ffn_conv_b"].rearrange("(k p) -> k p", p=128), 44, cb)
    gfr, fgr = A.f32(2048), A.f32(2048)
    brow2 = Buf()
    DMA(S, "sp", gfr, g.scr["gates"][1].rearrange("a b -> (a b)").partition_broadcast(128), [g.sb["gates"]], [brow2])
    DMA(S, "sp", fgr, I["final_norm_g"].partition_broadcast(128), [], [brow2])
    h2t = A.bf16(16 * 512)
    h23 = h2t.rearrange("p (k t) -> p k t", t=512)
    bh = Buf()
    gT = A.bf16(44 * 512)
    gT3 = gT.rearrange("p (f t) -> p f t", t=512)
    bgT = Buf()
    wuf = [A.f32(16 * 128) for _ in range(3)]
    bwuf = [Buf() for _ in range(3)]
    wub = [A.bf16(16 * 128) for _ in range(3)]
    bwub = [Buf() for _ in range(3)]
    wdf = [A.f32(512) for _ in range(3)]
    bwdf = [Buf() for _ in range(3)]
    wdb = [A.bf16(512) for _ in range(3)]
    bwdb = [Buf() for _ in range(3)]
    WUB = scratch(g, "WUB", [88, 128, 16 * 128], BF16)
    bWUB = [Buf() for _ in range(88)]
    WDB = scratch(g, "WDB", [176, 128, 512], BF16)
    bWDB = [Buf() for _ in range(176)]
    xs = [A.f32(2048) for _ in range(4)]
    bxs = [Buf() for _ in range(4)]
    cg, t5, gl = A.f32(512), A.f32(512), A.f32(512)
    bcg = Buf()
    junk = A.bf16(2048)
    bjunk = Buf()
    st = A.f32(8)
    bst_ = Buf()
    wuv = I["ffn_w_up"].rearrange("(kc p) n -> p kc n", p=128)
    nw = 0
    nd = 0
    for ti in range(4):
        DMA(S, "sp", h23, g.scr["H2T"].rearrange("k p t -> p k t")[:, :, ti * 512:(ti + 1) * 512], [g.sb["H2T"]], [bh])
        for su in range(4):
            r_ = ti * 512 + su * 128
            DMA(S, "pool", xs[su], X1[r_:r_ + 128, :], [g.sb["X1"]], [bxs[su]])
        for fc in range(44):
            pbs = []
            for half in range(2):
                c0 = half * DFF + fc * 128
                wb_ = wub[nw % 3]
                bwb_ = bwub[nw % 3]
                wb3 = wb_.rearrange("p (k m) -> p k m", m=128)
                wi = fc * 2 + half
                if ti == 0:
                    wf_ = wuf[nw % 3]
                    wf3 = wf_.rearrange("p (k m) -> p k m", m=128)
                    DMA(S, "sp" if nw % 2 == 0 else "pool", wf3, wuv[:, :, c0:c0 + 128], [], [bwuf[nw % 3]])
                    CP(S, "pool" if nw % 2 == 0 else "act", wb_, wf_, [bwuf[nw % 3]], [bwb_])
                    DMA(S, "pool" if nw % 2 == 0 else "sp", WUB[wi], wb_, [bwb_], [bWUB[wi]])
                else:
                    DMA(S, "sp" if nw % 2 == 0 else "pool", wb_, WUB[wi], [bWUB[wi]], [bwb_])
                pb = nw % 4
                for kc in range(16):
                    MM(S, bank[pb], wb3[:, kc, :], h23[:, kc, :], kc == 0, kc == 15, [bwb_, bh], [bk[pb]])
                pbs.append(pb)
                nw += 1
            pg, pv = pbs
            ACT(S, cg, bank[pg], AF.Identity, [bk[pg], g.b_pers], [bcg], bias=cb[:, fc:fc + 1], scale=cw[1][:, fc:fc + 1])
            c3 = cg.rearrange("p (r t) -> p r t", t=64)
            p3 = bank[pg].rearrange("p (r t) -> p r t", t=64)
            STT(S, "dve", c3[:, :, 1:64], p3[:, :, 0:63], cw[0][:, fc:fc + 1], c3[:, :, 1:64], ALU.mult, ALU.add,
                [bk[pg], g.b_pers, bcg], [bcg])
            STT(S, "dve", c3[:, :, 0:63], p3[:, :, 1:64], cw[2][:, fc:fc + 1], c3[:, :, 0:63], ALU.mult, ALU.add,
                [bk[pg], g.b_pers, bcg], [bcg])
            gelu_tanh(S, gl, cg, t5, bcg)
            TT(S, "dve", gT3[:, fc, :], gl, bank[pv], ALU.mult, [bcg, bk[pv]], [bgT])
        for db in range(4):
            dsl = slice(db * 512, (db + 1) * 512)
            for fc in range(44):
                wb_ = wdb[nd % 3]
                bwb_ = bwdb[nd % 3]
                wi = db * 44 + fc
                if ti == 0:
                    wf_ = wdf[nd % 3]
                    DMA(S, "sp" if nd % 2 == 0 else "pool", wf_, I["ffn_w_down"][fc * 128:(fc + 1) * 128, dsl], [], [bwdf[nd % 3]])
                    CP(S, "act" if nd % 2 == 0 else "pool", wb_, wf_, [bwdf[nd % 3]], [bwb_])
                    DMA(S, "pool" if nd % 2 == 0 else "sp", WDB[wi], wb_, [bwb_], [bWDB[wi]])
                else:
                    DMA(S, "sp" if nd % 2 == 0 else "pool", wb_, WDB[wi], [bWDB[wi]], [bwb_])
                for su in range(4):
                    MM(S, bank[4 + su], gT3[:, fc, su * 128:(su + 1) * 128], wb_, fc == 0, fc == 43,
                       [bgT, bwb_], [bk[4 + su]])
                nd += 1
            for su in range(4):
                TT(S, "dve", t5, bank[4 + su], gfr[:, dsl], ALU.mult, [bk[4 + su], brow2, bcg], [bcg])
                TT(S, "pool", xs[su][:, dsl], xs[su][:, dsl], t5, ALU.add, [bxs[su], bcg], [bxs[su]])
        for su in range(4):
            rstd_from(S, xs[su], st, bxs[su], bst_, junk, bjunk)
            TS(S, "dve", xs[su], xs[su], st[:, 3:4], None, ALU.mult, None, [bxs[su], bst_], [bxs[su]])
            TT(S, "pool", xs[su], xs[su], fgr, ALU.mult, [bxs[su], brow2], [bxs[su]])
            r_ = ti * 512 + su * 128
            DMA(S, "sp", g.out[r_:r_ + 128, :], xs[su], [bxs[su]], [Buf()])
```
